# Optimizing a Trainium2 kernel written in Bass

```python
import math
import jax, jax.numpy as jnp
from jax import lax
import numpy as np

D_MODEL = 2048
BATCH = 8
SEQ = 2048
DEPTH = 4

N_MIXERS = 2
N_SSD_LAYERS = (DEPTH + 1) // 2
N_SB_LAYERS = DEPTH // 2
PLE_DIM = 256
SSD_EXPAND = 2
SSD_D_INNER = SSD_EXPAND * D_MODEL
SSD_HEAD_DIM = 64
SSD_N_HEADS = SSD_D_INNER // SSD_HEAD_DIM
SSD_N_GROUPS = 8
SSD_HEADS_PER_GROUP = SSD_N_HEADS // SSD_N_GROUPS
SSD_D_STATE = 128
SSD_D_CONV = 4
SSD_CHUNK = 128
SSD_CONV_DIM = SSD_D_INNER + 2 * SSD_N_GROUPS * SSD_D_STATE
SSD_IN_DIM = SSD_D_INNER + SSD_CONV_DIM + SSD_N_HEADS
SB_HEAD_DIM = 128
SB_N_HEADS = D_MODEL // SB_HEAD_DIM
SB_WIDTH = SB_N_HEADS * SB_HEAD_DIM
SB_QBLOCK = 128
NORM_EPS = 1e-6
GATED_NORM_EPS = 1e-5

kernel_name = "ssd_stickbreaking_interleaved_ple"


def rms_norm(x, w, eps=NORM_EPS):
    xf = x.astype(jnp.float32)
    y = xf * lax.rsqrt(jnp.mean(xf * xf, axis=-1, keepdims=True) + eps)
    return (y * w.astype(jnp.float32)).astype(x.dtype)


def causal_depthwise_conv(x, w, b):
    y = lax.conv_general_dilated(
        x, w[:, None, :], window_strides=(1,), padding=[(SSD_D_CONV - 1, 0)],
        dimension_numbers=("NWC", "WIO", "NWC"), feature_group_count=x.shape[-1])
    return y + b


def ssd_chunked_scan(xh, dt, a, bm, cm):
    b, s = xh.shape[0], xh.shape[1]
    nc = s // SSD_CHUNK
    L = SSD_CHUNK
    xdt = (xh * dt[..., None]).reshape(b, nc, L, SSD_N_GROUPS, SSD_HEADS_PER_GROUP, SSD_HEAD_DIM)
    adt = (dt * a).reshape(b, nc, L, SSD_N_GROUPS, SSD_HEADS_PER_GROUP)
    bm = bm.reshape(b, nc, L, SSD_N_GROUPS, SSD_D_STATE)
    cm = cm.reshape(b, nc, L, SSD_N_GROUPS, SSD_D_STATE)
    acum = jnp.cumsum(adt, axis=2)
    seg = acum[:, :, :, None] - acum[:, :, None, :]
    causal = jnp.tril(jnp.ones((L, L), dtype=bool))[None, None, :, :, None, None]
    decay = jnp.exp(jnp.where(causal, seg, -jnp.inf))
    scores = jnp.einsum("bclgn,bcsgn->bclsg", cm, bm)
    y_diag = jnp.einsum("bclsg,bclsgr,bcsgrp->bclgrp", scores, decay, xdt)
    decay_to_end = jnp.exp(acum[:, :, -1:] - acum)
    chunk_states = jnp.einsum("bclgn,bclgr,bclgrp->bcgrpn", bm, decay_to_end, xdt)
    chunk_decay = jnp.exp(acum[:, :, -1])

    def step(state, inp):
        cs, cd = inp
        return state * cd[..., None, None] + cs, state

    init = jnp.zeros((b, SSD_N_GROUPS, SSD_HEADS_PER_GROUP, SSD_HEAD_DIM, SSD_D_STATE), jnp.float32)
    _, prev_states = lax.scan(step, init, (jnp.moveaxis(chunk_states, 1, 0), jnp.moveaxis(chunk_decay, 1, 0)))
    prev_states = jnp.moveaxis(prev_states, 0, 1)
    y_off = jnp.einsum("bclgn,bcgrpn,bclgr->bclgrp", cm, prev_states, jnp.exp(acum))
    return (y_diag + y_off).reshape(b, s, SSD_N_GROUPS, SSD_HEADS_PER_GROUP, SSD_HEAD_DIM)


def ssd_branch(u, in_w, conv_w, conv_b, dt_bias, a_log, d_skip, gnorm_w, out_w):
    b, s, _ = u.shape
    proj = u @ in_w
    z = proj[..., :SSD_D_INNER]
    xbc = proj[..., SSD_D_INNER:SSD_D_INNER + SSD_CONV_DIM]
    dt_raw = proj[..., SSD_D_INNER + SSD_CONV_DIM:]
    xbc = jax.nn.silu(causal_depthwise_conv(xbc, conv_w, conv_b))
    nbc = SSD_N_GROUPS * SSD_D_STATE
    xs = xbc[..., :SSD_D_INNER].astype(jnp.float32).reshape(b, s, SSD_N_GROUPS, SSD_HEADS_PER_GROUP, SSD_HEAD_DIM)
    bm = xbc[..., SSD_D_INNER:SSD_D_INNER + nbc].astype(jnp.float32).reshape(b, s, SSD_N_GROUPS, SSD_D_STATE)
    cm = xbc[..., SSD_D_INNER + nbc:].astype(jnp.float32).reshape(b, s, SSD_N_GROUPS, SSD_D_STATE)
    dt = jax.nn.softplus(dt_raw.astype(jnp.float32) + dt_bias.astype(jnp.float32))
    dt = dt.reshape(b, s, SSD_N_GROUPS, SSD_HEADS_PER_GROUP)
    a = (-jnp.exp(a_log.astype(jnp.float32))).reshape(SSD_N_GROUPS, SSD_HEADS_PER_GROUP)
    y = ssd_chunked_scan(xs, dt, a, bm, cm)
    y = y + d_skip.astype(jnp.float32).reshape(SSD_N_GROUPS, SSD_HEADS_PER_GROUP)[..., None] * xs
    y = y.reshape(b, s, SSD_D_INNER) * jax.nn.silu(z.astype(jnp.float32))
    yg = y.reshape(b, s, SSD_N_GROUPS, SSD_D_INNER // SSD_N_GROUPS)
    yg = yg * lax.rsqrt(jnp.mean(yg * yg, axis=-1, keepdims=True) + GATED_NORM_EPS)
    y = (yg.reshape(b, s, SSD_D_INNER) * gnorm_w.astype(jnp.float32)).astype(u.dtype)
    return y @ out_w


def stick_breaking_attention(q, k, v):
    s = q.shape[2]
    scale = 1.0 / math.sqrt(SB_HEAD_DIM)
    outs = []
    for blk in range(s // SB_QBLOCK):
        t0 = blk * SB_QBLOCK
        kend = t0 + SB_QBLOCK
        z = jnp.einsum("bhtd,bhsd->bhts", q[:, :, t0:kend], k[:, :, :kend]) * scale
        t_idx = t0 + jnp.arange(SB_QBLOCK)[:, None]
        s_idx = jnp.arange(kend)[None, :]
        strict = s_idx < t_idx
        log_beta = jax.nn.log_sigmoid(z)
        log_1m_beta = jnp.where(strict, jax.nn.log_sigmoid(-z), 0.0)
        rest = lax.cumsum(log_1m_beta, axis=3, reverse=True) - log_1m_beta
        att = jnp.where(strict, jnp.exp(log_beta + rest), 0.0)
        outs.append(jnp.einsum("bhts,bhsd->bhtd", att, v[:, :, :kend]))
    return jnp.concatenate(outs, axis=2)


def sb_branch(u, in_w, qn_w, kn_w, out_w):
    b, s, _ = u.shape
    proj = u @ in_w
    q, k, v, g = jnp.split(proj, 4, axis=-1)
    def heads(t):
        return t.reshape(b, s, SB_N_HEADS, SB_HEAD_DIM)
    q = rms_norm(heads(q), qn_w).astype(jnp.float32).transpose(0, 2, 1, 3)
    k = rms_norm(heads(k), kn_w).astype(jnp.float32).transpose(0, 2, 1, 3)
    v = heads(v).astype(jnp.float32).transpose(0, 2, 1, 3)
    o = stick_breaking_attention(q, k, v).transpose(0, 2, 1, 3).reshape(b, s, SB_WIDTH)
    o = (o * jax.nn.silu(g.astype(jnp.float32))).astype(u.dtype)
    return o @ out_w


def setup_inputs(seed: int = 0) -> dict:
    key = jax.random.key(seed)
    ks = jax.random.split(key, 24)
    f32 = jnp.float32
    nrm = lambda k, shape, sc: jax.random.normal(k, shape, f32) * sc
    dt0 = jnp.exp(jax.random.uniform(ks[7], (N_SSD_LAYERS, SSD_N_HEADS), f32)
                  * (math.log(0.1) - math.log(0.001)) + math.log(0.001))
    return {
        "x": nrm(ks[0], (BATCH, SEQ, D_MODEL), 1.0),
        "p": nrm(ks[1], (DEPTH, BATCH, SEQ, PLE_DIM), 1.0),
        "norm_w": 1.0 + nrm(ks[2], (DEPTH, D_MODEL), 0.02),
        "ssd_in_w": nrm(ks[3], (N_SSD_LAYERS, D_MODEL, SSD_IN_DIM), D_MODEL ** -0.5),
        "ssd_conv_w": nrm(ks[4], (N_SSD_LAYERS, SSD_D_CONV, SSD_CONV_DIM), SSD_D_CONV ** -0.5),
        "ssd_conv_b": nrm(ks[5], (N_SSD_LAYERS, SSD_CONV_DIM), 0.02),
        "ssd_dt_bias": dt0 + jnp.log(-jnp.expm1(-dt0)),
        "ssd_a_log": jnp.log(jax.random.uniform(ks[8], (N_SSD_LAYERS, SSD_N_HEADS), f32, 1.0, 16.0)),
        "ssd_d": 1.0 + nrm(ks[9], (N_SSD_LAYERS, SSD_N_HEADS), 0.02),
        "ssd_gnorm_w": 1.0 + nrm(ks[10], (N_SSD_LAYERS, SSD_D_INNER), 0.02),
        "ssd_out_w": nrm(ks[11], (N_SSD_LAYERS, SSD_D_INNER, D_MODEL), SSD_D_INNER ** -0.5),
        "sb_in_w": nrm(ks[12], (N_SB_LAYERS, D_MODEL, 4 * SB_WIDTH), D_MODEL ** -0.5),
        "sb_qn_w": 1.0 + nrm(ks[13], (N_SB_LAYERS, SB_HEAD_DIM), 0.02),
        "sb_kn_w": 1.0 + nrm(ks[14], (N_SB_LAYERS, SB_HEAD_DIM), 0.02),
        "sb_out_w": nrm(ks[15], (N_SB_LAYERS, SB_WIDTH, D_MODEL), SB_WIDTH ** -0.5),
        "ple_norm_w": 1.0 + nrm(ks[16], (DEPTH, D_MODEL), 0.02),
        "ple_gate_w": nrm(ks[17], (DEPTH, D_MODEL, D_MODEL), D_MODEL ** -0.5),
        "ple_proj_w": nrm(ks[18], (DEPTH, PLE_DIM, D_MODEL), 0.5 * PLE_DIM ** -0.5),
    }


def reference(x, p, norm_w, ssd_in_w, ssd_conv_w, ssd_conv_b, ssd_dt_bias, ssd_a_log, ssd_d,
              ssd_gnorm_w, ssd_out_w, sb_in_w, sb_qn_w, sb_kn_w, sb_out_w,
              ple_norm_w, ple_gate_w, ple_proj_w):
    h = x
    for i in range(DEPTH):
        u = rms_norm(h, norm_w[i])
        j = i // N_MIXERS
        if i % N_MIXERS == 0:
            mix = ssd_branch(u, ssd_in_w[j], ssd_conv_w[j], ssd_conv_b[j], ssd_dt_bias[j],
                             ssd_a_log[j], ssd_d[j], ssd_gnorm_w[j], ssd_out_w[j])
        else:
            mix = sb_branch(u, sb_in_w[j], sb_qn_w[j], sb_kn_w[j], sb_out_w[j])
        h = h + mix
        gate = jax.nn.sigmoid((rms_norm(h, ple_norm_w[i]) @ ple_gate_w[i]).astype(jnp.float32))
        h = h + ((p[i] @ ple_proj_w[i]).astype(jnp.float32) * gate).astype(h.dtype)
    return h
```

```python
import numpy as np
import concourse.bass as bass
import concourse.mybir as mybir
from concourse.bass_utils import run_bass_kernel_spmd

F32 = mybir.dt.float32
BF16 = mybir.dt.bfloat16
AF = mybir.ActivationFunctionType
ALU = mybir.AluOpType

EPOCH = 12000


class Res:
    __slots__ = ("name", "writers", "readers", "dsem", "dcount", "h")

    def __init__(self, name, h=None):
        self.name = name
        self.writers = []
        self.readers = []
        self.dsem = None
        self.dcount = 0
        self.h = h

    def ap(self):
        return self.h[:]


class Op:
    __slots__ = ("eng", "fn", "deps", "signal", "is_dma", "sem", "val", "key")

    def __init__(self, eng, fn, deps, is_dma):
        self.eng = eng
        self.fn = fn
        self.deps = deps
        self.signal = is_dma
        self.is_dma = is_dma
        self.sem = None
        self.val = 0
        self.key = eng


def _merge(lst, op):
    for i, o in enumerate(lst):
        if o.key == op.key:
            lst[i] = op
            return
    lst.append(op)


SB_LO = 16512
SB_HI = 229344


class StopBuild(Exception):
    pass


class Prog:
    ENGS = ("pe", "act", "dve", "pool", "sp")

    def __init__(self, nc):
        self.nc = nc
        self.streams = {e: [] for e in self.ENGS}
        self.nsem = 0
        self.nres = 0
        self.persist_ptr = SB_LO
        self.stage_base = SB_LO
        self.stage_ptr = SB_LO
        self.stage_res = []
        self.sem_pool = []
        self.last_dma = {}
        self.last_cmp = {}
        self.wcache = {}

    def _alloc(self, name, shape, dtype, off):
        self.nres += 1
        return self.nc.alloc_sbuf_tensor_at(f"{name}_{self.nres}", list(shape), dtype, offset=off)

    @staticmethod
    def _bytes(shape, dtype):
        n = 1
        for s in shape[1:]:
            n *= s
        n *= 2 if dtype == BF16 else 4
        return (n + 63) // 64 * 64

    def sbp(self, name, shape, dtype):
        off = self.persist_ptr
        self.persist_ptr += self._bytes(shape, dtype)
        assert self.persist_ptr <= SB_HI
        self.stage_base = self.persist_ptr
        self.stage_ptr = self.persist_ptr
        return Res(name, self._alloc(name, shape, dtype, off))

    def sb(self, name, shape, dtype):
        off = self.stage_ptr
        self.stage_ptr += self._bytes(shape, dtype)
        assert self.stage_ptr <= SB_HI, f"SBUF overflow at {name}: {self.stage_ptr}"
        r = Res(name, self._alloc(name, shape, dtype, off))
        self.stage_res.append(r)
        return r

    def ps(self, name, shape, dtype=F32):
        self.nres += 1
        return Res(name, self.nc.alloc_psum_tensor(f"{name}_{self.nres}", list(shape), dtype))

    def new_sem(self, name):
        self.nsem += 1
        return self.nc.alloc_semaphore(name=f"{name}_{self.nsem}")

    def stage_begin(self, base=None):
        self.nstage = getattr(self, "nstage", 0) + 1
        if self.nstage > getattr(self, "max_stages", 10 ** 9):
            raise StopBuild()
        self.barrier()
        for r in self.stage_res:
            if r.dsem is not None:
                self.sem_pool.append((r.dsem, r.dcount))
                r.dsem = None
        self.stage_res = []
        self.wcache = {}
        self.stage_ptr = self.stage_base if base is None else base

    def barrier(self):
        deps = list(self.last_cmp.values()) + list(self.last_dma.values())
        if not deps:
            return
        for e in self.ENGS:
            self.streams[e].append(Op(e, None, list(deps), False))

    def _deps(self, eng, reads, writes, acc):
        deps = []
        seen = set()

        def add(o):
            if id(o) in seen:
                return
            if o.eng == "pe" and eng == "pe" and not o.is_dma:
                return
            seen.add(id(o))
            deps.append(o)

        for r in reads:
            for o in r.writers:
                add(o)
        for w in writes:
            for o in w.writers:
                add(o)
            for o in w.readers:
                add(o)
        for w in acc:
            if w.readers:
                for o in w.writers:
                    add(o)
                for o in w.readers:
                    add(o)
        return deps

    def _commit(self, op, reads, writes, acc):
        for w in acc:
            if w.readers:
                w.writers = [op]
                w.readers = []
            else:
                _merge(w.writers, op)
        for w in writes:
            w.writers = [op]
            w.readers = []
        for r in reads:
            _merge(r.readers, op)

    def op(self, eng, fn, reads=(), writes=(), acc=()):
        o = Op(eng, fn, self._deps(eng, reads, writes, acc), False)
        self.streams[eng].append(o)
        self._commit(o, reads, writes, acc)
        self.last_cmp[eng] = o
        return o

    def dma(self, eng, out_ap, in_ap, reads=(), writes=(), acc=()):
        dst = (list(writes) + list(acc))[0]
        if dst.dsem is None:
            if self.sem_pool:
                dst.dsem, dst.dcount = self.sem_pool.pop()
            else:
                dst.dsem, dst.dcount = self.new_sem("d"), 0
        o = Op(eng, None, self._deps(eng, reads, writes, acc), True)
        dst.dcount += 16
        o.sem = dst.dsem
        o.val = dst.dcount
        o.key = ("d", id(dst.dsem))
        o.fn = lambda e, a=out_ap, b=in_ap: e.dma_start(out=a, in_=b)
        self.streams[eng].append(o)
        self._commit(o, reads, writes, acc)
        self.last_dma[id(dst.dsem)] = o
        return o

    def wait(self, eng, ress):
        deps = []
        for r in ress:
            deps += r.writers
        o = Op(eng, None, deps, False)
        self.streams[eng].append(o)
        return o

    def mm(self, out, lhsT, rhs, start, stop, reads, writes):
        return self.op("pe", lambda e: e.matmul(out, lhsT, rhs, start=start, stop=stop), reads, writes)

    def tr(self, out, in_, ident, reads, writes):
        return self.op("pe", lambda e: e.transpose(out, in_, ident), reads, writes)

    def act(self, out, in_, func, reads=(), writes=(), acc=(), **kw):
        return self.op("act", lambda e: e.activation(out, in_, func, **kw), reads, writes, acc)

    def tt(self, eng, out, a, b, op, reads=(), writes=(), acc=()):
        return self.op(eng, lambda e: e.tensor_tensor(out, a, b, op), reads, writes, acc)

    def stt(self, eng, out, in0, scalar, in1, op0, op1, reads=(), writes=(), acc=()):
        return self.op(eng, lambda e: e.scalar_tensor_tensor(out, in0, scalar, in1, op0, op1), reads, writes, acc)

    def ts(self, eng, out, in0, s1, s2, op0, op1, reads=(), writes=(), acc=()):
        return self.op(eng, lambda e: e.tensor_scalar(out, in0, s1, s2, op0, op1), reads, writes, acc)

    def cp(self, eng, out, in_, reads=(), writes=(), acc=()):
        if eng == "act":
            return self.op(eng, lambda e: e.activation(out, in_, AF.Copy), reads, writes, acc)
        return self.op(eng, lambda e: e.tensor_copy(out, in_), reads, writes, acc)

    def memset(self, eng, out, val, writes=(), acc=()):
        return self.op(eng, lambda e: e.memset(out, val), (), writes, acc)

    def emit(self):
        nc = self.nc
        for e in self.ENGS:
            for o in self.streams[e]:
                for d in o.deps:
                    d.signal = True
        for e in self.ENGS:
            cnt = 0
            sem = None
            for o in self.streams[e]:
                if o.is_dma or not o.signal or o.fn is None:
                    continue
                if sem is None or cnt >= EPOCH:
                    sem = self.new_sem("c_" + e)
                    cnt = 0
                cnt += 1
                o.sem = sem
                o.val = cnt

        def run(ename, eng):
            waited = {}
            for o in self.streams[ename]:
                for d in o.deps:
                    k = id(d.sem)
                    if waited.get(k, 0) >= d.val:
                        continue
                    eng.wait_ge(d.sem, d.val)
                    waited[k] = d.val
                if o.fn is None:
                    continue
                ins = o.fn(eng)
                if o.signal:
                    ins.then_inc(o.sem, 16 if o.is_dma else 1)

        with nc.Block() as block:
            @block.tensor
            def _(eng):
                run("pe", eng)

            @block.scalar
            def _(eng):
                run("act", eng)

            @block.vector
            def _(eng):
                run("dve", eng)

            @block.gpsimd
            def _(eng):
                run("pool", eng)

            @block.sync
            def _(eng):
                run("sp", eng)


import os
SCAN_CUT = float(os.environ.get("SCAN_CUT", "9"))
D = 2048
KC = 16
EPS = 1e-6
GEPS = 1e-5
NEG = -30000.0


def build(S=2048, layers=(0, 1, 2, 3), dbg=False, max_stages=10 ** 9):
    NB = S // 128
    nc = bass.Bass("TRN2", target_bir_lowering=False)
    P = Prog(nc)
    P.max_stages = max_stages

    def din(name, shape, dt=F32):
        return nc.dram_tensor(name, list(shape), dt, kind="ExternalInput").ap()

    skind = "ExternalOutput" if dbg else "Internal"

    def dscr(name, shape, dt=BF16):
        return nc.dram_tensor(name, list(shape), dt, kind=skind).ap()

    xT = din("xT", [D, S])
    pT = din("pT", [4, 256, S])
    normw = din("normw", [128, 64])
    plenw = din("plenw", [128, 64])
    convw = din("convw", [128, 384])
    convb = din("convb", [128, 96])
    dtb = din("dtb", [128, 128])
    alog = din("alog", [128, 128])
    dskip = din("dskip", [128, 128])
    gnw = din("gnw", [128, 8192])
    qnw = din("qnw", [128, 2])
    knw = din("knw", [128, 2])
    ssd_in_w = din("ssd_in_w", [2, D, 10304])
    ssd_out_w = din("ssd_out_w", [2, 4096, D])
    sb_in_w = din("sb_in_w", [2, D, 8192])
    sb_out_w = din("sb_out_w", [2, D, D])
    gate_w = din("gate_w", [4, D, D])
    pproj_w = din("pproj_w", [4, 256, D])
    out = nc.dram_tensor("out", [D, S], F32, kind="ExternalOutput").ap()
    win = Res("win")
    outR = Res("out")

    hA = dscr("hA", [D, S], F32); hAR = Res("hA")
    hB = dscr("hB", [D, S], F32); hBR = Res("hB")
    zs_tok = dscr("zs_tok", [S, 4096]); zsR = Res("zs")
    xs_tok = dscr("xs_tok", [S, 4096]); xsR = Res("xs")
    B_tok = dscr("B_tok", [S, 1024]); BtR = Res("Bt")
    B_T = dscr("B_T", [1024, S]); BTR = Res("BT")
    C_T = dscr("C_T", [1024, S]); CTR = Res("CT")
    yT = dscr("yT", [4096, S]); yTR = Res("yT")
    qT = dscr("qT", [D, S]); qR = Res("q")
    kT = dscr("kT", [D, S]); kR = Res("k")
    v_tok = dscr("v_tok", [S, D]); vR = Res("v")
    sgT = dscr("sgT", [D, S]); sgR = Res("sg")

    psb = [P.ps(f"psb{i}", [128, 512], F32) for i in range(7)]
    pst = P.ps("pst", [128, 1024], BF16)

    ones_bf = P.sbp("ones_bf", [128, 128], BF16)
    ident_bf = P.sbp("ident_bf", [128, 128], BF16)
    tincl_bf = P.sbp("tincl_bf", [128, 128], BF16)
    ones_f = P.sbp("ones_f", [128, 128], F32)
    ident_f = P.sbp("ident_f", [128, 128], F32)
    tri_f = P.sbp("tri_f", [128, 128], F32)
    neg4 = P.sbp("neg4", [128, 512], F32)
    maskr = [P.sbp(f"mask{r}", [128, 512], BF16) for r in range(4)]
    normw_sb = P.sbp("normw_sb", [128, 64], F32)
    plenw_sb = P.sbp("plenw_sb", [128, 64], F32)
    convw_sb = P.sbp("convw_sb", [128, 384], F32)
    convb_sb = P.sbp("convb_sb", [128, 96], F32)
    dtb_sb = P.sbp("dtb_sb", [128, 128], F32)
    a_sb = P.sbp("a_sb", [128, 128], F32)
    dskip_sb = P.sbp("dskip_sb", [128, 128], F32)
    qnw_sb = P.sbp("qnw_sb", [128, 2], F32)
    knw_sb = P.sbp("knw_sb", [128, 2], F32)
    dtt = P.sbp("dtt", [128, NB, 64], F32)
    U_OFF = P.persist_ptr
    U = P.sbp("U", [128, 16 * S], BF16)

    def sel(res, pattern, op, fill, base, cm):
        P.op("pool", lambda e: e.affine_select(res.ap(), res.ap(), pattern=pattern, compare_op=op, fill=fill,
                                                base=base, channel_multiplier=cm), reads=[res], writes=[res])

    P.memset("pool", ones_bf.ap(), 1.0, writes=[ones_bf])
    P.memset("pool", ones_f.ap(), 1.0, writes=[ones_f])
    P.memset("pool", ident_bf.ap(), 1.0, writes=[ident_bf])
    sel(ident_bf, [[-1, 128]], ALU.is_equal, 0.0, 0, 1)
    P.memset("pool", ident_f.ap(), 1.0, writes=[ident_f])
    sel(ident_f, [[-1, 128]], ALU.is_equal, 0.0, 0, 1)
    P.memset("pool", tincl_bf.ap(), 1.0, writes=[tincl_bf])
    sel(tincl_bf, [[-1, 128]], ALU.is_ge, 0.0, 0, 1)
    P.memset("pool", tri_f.ap(), 1.0, writes=[tri_f])
    sel(tri_f, [[1, 128]], ALU.is_ge, 0.0, 0, -1)
    P.memset("pool", neg4.ap(), NEG, writes=[neg4])
    P.op("pool", lambda e: e.affine_select(neg4.ap().rearrange("p (a b) -> p a b", a=4),
                                            neg4.ap().rearrange("p (a b) -> p a b", a=4),
                                            pattern=[[0, 4], [-1, 128]], compare_op=ALU.is_gt, fill=0.0, base=0,
                                            channel_multiplier=1), reads=[neg4], writes=[neg4])
    for r in range(4):
        P.memset("pool", maskr[r].ap(), 1.0, writes=[maskr[r]])
        sel(maskr[r], [[1, 512]], ALU.is_gt, 0.0, -128 * r, -1)
    for sbt, dr in ((normw_sb, normw), (plenw_sb, plenw), (convw_sb, convw), (convb_sb, convb), (dtb_sb, dtb),
                    (a_sb, alog), (dskip_sb, dskip), (qnw_sb, qnw), (knw_sb, knw)):
        P.dma("sp", sbt.ap(), dr, reads=[win], writes=[sbt])
    P.act(a_sb.ap(), a_sb.ap(), AF.Exp, reads=[a_sb], writes=[a_sb])
    P.ts("dve", a_sb.ap(), a_sb.ap(), -1.0, 0.0, ALU.mult, ALU.add, reads=[a_sb], writes=[a_sb])
    P.ts("dve", qnw_sb.ap(), qnw_sb.ap(), float(128 ** -0.5), 0.0, ALU.mult, ALU.add, reads=[qnw_sb], writes=[qnw_sb])

    Uv16 = U.ap().rearrange("p (k t) -> p k t", k=16)

    def norm_stage(h_ap, h_res, wsb, wofs):
        P.stage_begin()
        TT = 256
        Hb = [P.sb("Hb", [128, 16, TT], F32) for _ in range(2)]
        sq = [P.sb("sq", [128, 16, TT], BF16) for _ in range(2)]
        rs = [P.sb("rs", [128, TT], F32) for _ in range(2)]
        hv = h_ap.rearrange("(kc p) t -> p kc t", p=128)
        for tt in range(S // TT):
            b = tt % 2
            sl = slice(tt * TT, (tt + 1) * TT)
            P.dma("sp", Hb[b].ap(), hv[:, :, sl], reads=[h_res], writes=[Hb[b]])
            P.act(sq[b].ap(), Hb[b].ap(), AF.Square, reads=[Hb[b]], writes=[sq[b]])
            ps = psb[b]
            for kc in range(16):
                P.mm(ps.ap()[:, :TT], ones_bf.ap(), sq[b].ap()[:, kc, :], kc == 0, kc == 15,
                     reads=[sq[b], ones_bf], writes=[ps])
            P.act(rs[b].ap(), ps.ap()[:, :TT], AF.Ln, reads=[ps], writes=[rs[b]], scale=1.0 / D, bias=EPS)
            P.act(rs[b].ap(), rs[b].ap(), AF.Exp, reads=[rs[b]], writes=[rs[b]], scale=-0.5)
            for kc in range(16):
                P.stt("dve", Uv16[:, kc, sl], Hb[b].ap()[:, kc, :], wsb.ap()[:, wofs + kc:wofs + kc + 1], rs[b].ap(),
                      ALU.mult, ALU.mult, reads=[Hb[b], wsb, rs[b]], acc=[U])

    pcnt = [0]

    def proj(Av, Ares, kc_n, T, w_ap, F, mode, evac, pre=None, FB=None):
        if FB is None:
            FB = min(F, 4096 // kc_n)
        key = kc_n * FB
        NWB = 3
        if key not in P.wcache:
            P.wcache[key] = [P.sb("wbf", [128, key], BF16) for _ in range(NWB)]
        wbf = P.wcache[key]
        wv = w_ap.rearrange("(kc p) f -> p kc f", p=128)
        nblk = F // FB
        TW = min(512, T)

        def load(i):
            b = i % NWB
            P.dma("pool", wbf[b].ap().rearrange("p (k f) -> p k f", k=kc_n), wv[:, :, i * FB:(i + 1) * FB],
                  reads=[win], writes=[wbf[b]])

        load(0)
        if nblk > 1:
            load(1)
        for i in range(nblk):
            if i + 2 < nblk:
                load(i + 2)
            wb = wbf[i % NWB]
            wbv = wb.ap().rearrange("p (k f) -> p k f", k=kc_n)
            f0 = i * FB
            if mode == "F":
                for sub in range(FB // 128):
                    for tt in range(T // TW):
                        ps = psb[pcnt[0] % 4]
                        pcnt[0] += 1
                        if pre is not None:
                            pre(f0 + sub * 128, tt, TW)
                        for kc in range(kc_n):
                            P.mm(ps.ap()[:, :TW], wbv[:, kc, sub * 128:(sub + 1) * 128], Av[:, kc, tt * TW:(tt + 1) * TW],
                                 kc == 0, kc == kc_n - 1, reads=[wb, Ares], writes=[ps])
                        evac(f0 + sub * 128, tt, ps, TW)
            else:
                for tb in range(T // 128):
                    ps = psb[pcnt[0] % 4]
                    pcnt[0] += 1
                    for kc in range(kc_n):
                        P.mm(ps.ap()[:, :FB], Av[:, kc, tb * 128:(tb + 1) * 128], wbv[:, kc, :],
                             kc == 0, kc == kc_n - 1, reads=[wb, Ares], writes=[ps])
                    evac(f0, tb, ps, FB)

    ecnt = [0]

    def resid_hooks(h_src, hsR, h_dst, hdR, t_off, extra=None):
        hb = [P.sb("hb", [128, 512], F32) for _ in range(4)]
        cur = {}

        def pre(f0, tt, TW):
            b = hb[ecnt[0] % 4]
            ecnt[0] += 1
            cur[(f0, tt)] = b
            t0 = t_off + tt * TW
            P.dma("act", b.ap()[:, :TW], h_src[f0:f0 + 128, t0:t0 + TW], reads=[hsR], writes=[b])

        def evac(f0, tt, ps, TW):
            b = cur.pop((f0, tt))
            t0 = t_off + tt * TW
            if extra is None:
                P.tt("dve", b.ap()[:, :TW], ps.ap()[:, :TW], b.ap()[:, :TW], ALU.add, reads=[ps, b], writes=[b])
            else:
                extra(f0, tt, ps, TW, b)
            P.dma("sp", h_dst[f0:f0 + 128, t0:t0 + TW], b.ap()[:, :TW], reads=[b], acc=[hdR])

        return pre, evac

    def transposes_to_tok(src, dst_ap, dstR, c0, xtrb, kidx):
        xtr = xtrb[kidx % 2]
        for g8 in range(NB // 8):
            for i in range(8):
                tb = g8 * 8 + i
                P.tr(pst.ap()[:, i * 128:(i + 1) * 128], src.ap()[:, tb * 128:(tb + 1) * 128], ident_bf.ap(),
                     reads=[src, ident_bf], writes=[pst])
            P.cp("dve", xtr.ap()[:, g8 * 8:(g8 + 1) * 8, :], pst.ap().rearrange("p (a b) -> p a b", a=8),
                 reads=[pst], acc=[xtr])
        P.dma("sp", dst_ap.rearrange("(tb p) c -> p tb c", p=128)[:, :, c0:c0 + 128], xtr.ap(), reads=[xtr], acc=[dstR])

    def ssd_layer(j, h_in, hinR, h_out, houtR, nofs):
        norm_stage(h_in, hinR, normw_sb, nofs)
        P.stage_begin()
        w = ssd_in_w[j]
        zst = [P.sb("zst", [128, NB, 256], BF16) for _ in range(2)]

        def evac_z(f0, tb, ps, FB):
            buf = zst[(f0 // 256) % 2]
            P.act(buf.ap()[:, tb, :], ps.ap()[:, :256], AF.Silu, reads=[ps], acc=[buf])
            if tb == NB - 1:
                P.dma("sp", zs_tok.rearrange("(tb p) f -> p tb f", p=128)[:, :, f0:f0 + 256], buf.ap(), reads=[buf], acc=[zsR])

        proj(Uv16, U, 16, S, w[:, 0:4096], 4096, "T", evac_z)

        P.stage_begin()
        xraw = [P.sb("xraw", [128, 4 + S], BF16) for _ in range(2)]
        xcb = [P.sb("xcb", [128, S], BF16) for _ in range(2)]
        xtrb = [P.sb("xtrb", [128, NB, 128], BF16) for _ in range(2)]
        dgb = [P.sb("dgb", [128, 4, 128], BF16) for _ in range(2)]
        for b in range(2):
            P.memset("pool", xraw[b].ap()[:, 0:4], 0.0, acc=[xraw[b]])
        cc = [0]
        pend = []

        def evac_c(f0, tt, ps, TW):
            blk = f0 // 128
            xr = xraw[blk % 2]
            xc = xcb[blk % 2]
            dg = dgb[blk % 2]
            wi = (j * 48 + blk) * 4
            if tt == 0:
                for k in range(4):
                    P.ts("pool", dg.ap()[:, k, :], ident_f.ap(), convw_sb.ap()[:, wi + k:wi + k + 1], 0.0, ALU.mult, ALU.add,
                         reads=[ident_f, convw_sb], acc=[dg])
            P.cp("act", xr.ap()[:, 4 + tt * TW:4 + (tt + 1) * TW], ps.ap()[:, :TW], reads=[ps], acc=[xr])
            while pend:
                pend.pop(0)()
            pend.append(lambda: conv_tile(blk, tt, TW))

        def conv_tile(blk, tt, TW):
            xr = xraw[blk % 2]
            xc = xcb[blk % 2]
            dg = dgb[blk % 2]
            ps2 = psb[4 + cc[0] % 2]
            cc[0] += 1
            for k in range(4):
                sh = 3 - k
                P.mm(ps2.ap()[:, :TW], dg.ap()[:, k, :], xr.ap()[:, 4 + tt * TW - sh:4 + (tt + 1) * TW - sh], k == 0, k == 3,
                     reads=[dg, xr], writes=[ps2])
            P.act(xc.ap()[:, tt * TW:(tt + 1) * TW], ps2.ap()[:, :TW], AF.Silu, reads=[ps2, convb_sb], acc=[xc],
                  bias=convb_sb.ap()[:, j * 48 + blk:j * 48 + blk + 1])
            if tt != S // TW - 1:
                return
            if blk >= 40:
                P.dma("sp", C_T[(blk - 40) * 128:(blk - 39) * 128, :], xc.ap(), reads=[xc], acc=[CTR])
            elif blk >= 32:
                P.dma("sp", B_T[(blk - 32) * 128:(blk - 31) * 128, :], xc.ap(), reads=[xc], acc=[BTR])
                transposes_to_tok(xc, B_tok, BtR, (blk - 32) * 128, xtrb, blk)
            else:
                transposes_to_tok(xc, xs_tok, xsR, blk * 128, xtrb, blk)

        proj(Uv16, U, 16, S, w[:, 4096:10240], 6144, "F", evac_c)
        while pend:
            pend.pop(0)()

        def evac_dt(f0, tb, ps, FB):
            P.tt("dve", dtt.ap()[:, tb, :], ps.ap()[:, :64], dtb_sb.ap()[:, j * 64:(j + 1) * 64], ALU.add,
                 reads=[ps, dtb_sb], acc=[dtt])

        proj(Uv16, U, 16, S, w[:, 10240:10304], 64, "T", evac_dt, FB=64)
        P.act(dtt.ap(), dtt.ap(), AF.Exp, reads=[dtt], writes=[dtt])
        P.act(dtt.ap(), dtt.ap(), AF.Ln, reads=[dtt], writes=[dtt], bias=1.0)

        P.stage_begin(base=U_OFF)
        gnw_t = P.sb("gnw_t", [128, 4096], BF16)
        P.dma("pool", gnw_t.ap(), gnw[:, j * 4096:(j + 1) * 4096], reads=[win], writes=[gnw_t])
        state_f = [P.sb("state_f", [128, 512], F32) for _ in range(8)]
        state_b = [P.sb("state_b", [128, 512], BF16) for _ in range(8)]
        for g in range(8):
            P.memset("pool", state_f[g].ap(), 0.0, writes=[state_f[g]])
            P.memset("pool", state_b[g].ap(), 0.0, writes=[state_b[g]])
        xs_c = [P.sb("xs_c", [128, 4096], BF16) for _ in range(2)]
        zs_c = [P.sb("zs_c", [128, 4096], BF16) for _ in range(2)]
        Bt_c = [P.sb("Bt_c", [128, 1024], BF16) for _ in range(2)]
        BT_c = [P.sb("BT_c", [128, 8, 128], BF16) for _ in range(2)]
        CT_c = [P.sb("CT_c", [128, 8, 128], BF16) for _ in range(2)]
        adt2 = [P.sb("adt", [128, 64], F32) for _ in range(2)]
        acum2 = [P.sb("acum", [128, 64], F32) for _ in range(2)]
        nacum2 = [P.sb("nacum", [128, 64], F32) for _ in range(2)]
        eacum2 = [P.sb("eacum", [128, 64], F32) for _ in range(2)]
        dte2 = [P.sb("dte", [128, 64], F32) for _ in range(2)]
        cdec2 = [P.sb("cdec", [128, 64], F32) for _ in range(2)]
        acT_hi2 = [P.sb("acT_hi", [64, 128], BF16) for _ in range(2)]
        acT_lo2 = [P.sb("acT_lo", [64, 128], BF16) for _ in range(2)]
        acT_t = P.sb("acT_t", [64, 128], F32)
        bdh = [P.sb("bdh", [64, 8, 128], BF16) for _ in range(4)]
        bdl = [P.sb("bdl", [64, 8, 128], BF16) for _ in range(4)]
        neg4b = P.sb("neg4b", [128, 512], BF16)
        P.cp("pool", neg4b.ap(), neg4.ap(), reads=[neg4], writes=[neg4b])
        sc2 = [P.sb("sc", [128, 8, 128], F32) for _ in range(2)]
        dec = [P.sb("dec", [128, 4, 128], F32) for _ in range(2)]
        Mb = [P.sb("Mb", [128, 4, 128], BF16) for _ in range(2)]
        xdtg = [P.sb("xdtg", [128, 512], BF16) for _ in range(8)]
        xdteg = [P.sb("xdteg", [128, 512], BF16) for _ in range(8)]
        dxg = [P.sb("dxg", [128, 512], BF16) for _ in range(8)]
        yoffs = [P.sb("yoffs", [128, 512], F32) for _ in range(2)]
        ygb = [P.sb("ygb", [128, 512], F32) for _ in range(3)]
        sqg = [P.sb("sqg", [128, 512], BF16) for _ in range(2)]
        ssqb = [P.sb("ssqb", [128, 1], F32) for _ in range(3)]
        ynb = [P.sb("ynb", [128, 512], BF16) for _ in range(3)]
        kkc = [0]
        yts = P.sb("yTst", [128, 32, 128], BF16)
        ps_a, ps_s, ps_st = psb[6], psb[1], psb[6]
        ps_seg = [psb[2], psb[3]]
        ps_yl = [psb[4], psb[0]]
        ps_ol = [psb[5], psb[1]]
        v3 = lambda ap_: ap_.rearrange("p (h d) -> p h d", d=64)
        bc = lambda ap_, n: ap_.unsqueeze(2).to_broadcast([128, ap_.shape[1], n])
        hs = slice(j * 64, (j + 1) * 64)

        def load_chunk(c):
            b = c % 2
            r = slice(c * 128, (c + 1) * 128)
            P.dma("sp", xs_c[b].ap(), xs_tok[r, :], reads=[xsR], writes=[xs_c[b]])
            P.dma("sp", Bt_c[b].ap(), B_tok[r, :], reads=[BtR], writes=[Bt_c[b]])
            P.dma("sp", BT_c[b].ap(), B_T.rearrange("(g n) t -> n g t", n=128)[:, :, r], reads=[BTR], writes=[BT_c[b]])
            P.dma("sp", CT_c[b].ap(), C_T.rearrange("(g n) t -> n g t", n=128)[:, :, r], reads=[CTR], writes=[CT_c[b]])
            P.dma("sp", zs_c[b].ap(), zs_tok[r, :], reads=[zsR], writes=[zs_c[b]])

        def prologue(c):
            adt, acum, nacum, eacum, dte, cdec = (adt2[c % 2], acum2[c % 2], nacum2[c % 2], eacum2[c % 2], dte2[c % 2], cdec2[c % 2])
            acT_hi, acT_lo, sc = acT_hi2[c % 2], acT_lo2[c % 2], sc2[c % 2]
            b = c % 2
            BT_, CT_ = BT_c[b], CT_c[b]
            P.tt("dve", adt.ap(), dtt.ap()[:, c, :], a_sb.ap()[:, hs], ALU.mult, reads=[dtt, a_sb], writes=[adt])
            P.mm(ps_a.ap()[:, 0:64], tri_f.ap(), adt.ap(), True, True, reads=[tri_f, adt], writes=[ps_a])
            P.mm(ps_a.ap()[:, 64:128], ones_f.ap(), adt.ap(), True, True, reads=[ones_f, adt], writes=[ps_a])
            P.cp("dve", acum.ap(), ps_a.ap()[:, 0:64], reads=[ps_a], writes=[acum])
            P.mm(ps_a.ap()[0:64, 128:256], acum.ap(), ident_f.ap(), True, True, reads=[acum, ident_f], writes=[ps_a])
            P.ts("dve", nacum.ap(), ps_a.ap()[:, 0:64], -1.0, 0.0, ALU.mult, ALU.add, reads=[ps_a], writes=[nacum])
            P.cp("dve", acT_hi.ap(), ps_a.ap()[0:64, 128:256], reads=[ps_a], writes=[acT_hi])
            P.tt("dve", acT_t.ap(), ps_a.ap()[0:64, 128:256], acT_hi.ap(), ALU.subtract, reads=[ps_a, acT_hi], writes=[acT_t])
            P.cp("dve", acT_lo.ap(), acT_t.ap(), reads=[acT_t], writes=[acT_lo])
            P.cp("dve", cdec.ap(), ps_a.ap()[:, 64:128], reads=[ps_a], writes=[cdec])
            P.tt("dve", dte.ap(), cdec.ap(), acum.ap(), ALU.subtract, reads=[cdec, acum], writes=[dte])
            P.act(eacum.ap(), acum.ap(), AF.Exp, reads=[acum], writes=[eacum])
            P.act(cdec.ap(), cdec.ap(), AF.Exp, reads=[cdec], writes=[cdec])
            P.act(dte.ap(), dte.ap(), AF.Exp, reads=[dte], writes=[dte])
            for g4 in range(2):
                for gi in range(4):
                    g = g4 * 4 + gi
                    P.mm(ps_s.ap()[:, gi * 128:(gi + 1) * 128], BT_.ap()[:, g, :], CT_.ap()[:, g, :], True, True,
                         reads=[BT_, CT_], writes=[ps_s])
                P.cp("dve", sc.ap()[:, g4 * 4:(g4 + 1) * 4, :], ps_s.ap().rearrange("p (a b) -> p a b", a=4),
                     reads=[ps_s], acc=[sc])

        def stP(c, g):
            adt, acum, nacum, eacum, dte, cdec = (adt2[c % 2], acum2[c % 2], nacum2[c % 2], eacum2[c % 2], dte2[c % 2], cdec2[c % 2])
            acT_hi, acT_lo, sc = acT_hi2[c % 2], acT_lo2[c % 2], sc2[c % 2]
            xs_ = xs_c[c % 2]
            gs = slice(g * 512, (g + 1) * 512)
            g8 = slice(g * 8, (g + 1) * 8)
            for src_, dst_ in ((acT_hi, bdh[g % 4]), (acT_lo, bdl[g % 4])):
                P.op("pool", lambda e, s_=src_, d_=dst_, g=g: e.affine_select(
                    d_.ap(), s_.ap().unsqueeze(1).to_broadcast([64, 8, 128]), pattern=[[-1, 8], [0, 128]],
                    compare_op=ALU.is_equal, fill=0.0, base=-8 * g, channel_multiplier=1), reads=[src_], writes=[dst_])
            P.tt("pool", v3(xdtg[g].ap()), v3(xs_.ap()[:, gs]), bc(dtt.ap()[:, c, g8], 64), ALU.mult,
                 reads=[xs_, dtt], writes=[xdtg[g]])
            P.tt("pool", v3(xdteg[g].ap()), v3(xdtg[g].ap()), bc(dte.ap()[:, g8], 64), ALU.mult,
                 reads=[xdtg[g], dte], writes=[xdteg[g]])
            P.tt("pool", v3(dxg[g].ap()), v3(xs_.ap()[:, gs]), bc(dskip_sb.ap()[:, j * 64 + g * 8:j * 64 + (g + 1) * 8], 64),
                 ALU.mult, reads=[xs_, dskip_sb], writes=[dxg[g]])

        def stA_seg(c, g):
            for hb_ in range(2):
                pseg = ps_seg[hb_]
                P.mm(pseg.ap(), ones_bf.ap()[0:64, :], bdh[g % 4].ap()[:, hb_ * 4:(hb_ + 1) * 4, :], True, False,
                     reads=[ones_bf, bdh[g % 4]], writes=[pseg])
                P.mm(pseg.ap(), ones_bf.ap()[0:64, :], bdl[g % 4].ap()[:, hb_ * 4:(hb_ + 1) * 4, :], False, False,
                     reads=[ones_bf, bdl[g % 4]], writes=[pseg])
                P.mm(pseg.ap(), ident_bf.ap(), neg4b.ap(), False, True, reads=[ident_bf, neg4b], writes=[pseg])

        def stA_exp(c, g):
            adt, acum, nacum, eacum, dte, cdec = (adt2[c % 2], acum2[c % 2], nacum2[c % 2], eacum2[c % 2], dte2[c % 2], cdec2[c % 2])
            acT_hi, acT_lo, sc = acT_hi2[c % 2], acT_lo2[c % 2], sc2[c % 2]
            for hb_ in range(2):
                pseg = ps_seg[hb_]
                h0 = g * 8 + hb_ * 4
                d_, m_ = dec[hb_], Mb[hb_]
                for hh in range(4):
                    h = h0 + hh
                    P.act(d_.ap()[:, hh, :], pseg.ap()[:, hh * 128:(hh + 1) * 128], AF.Exp, reads=[pseg, nacum], acc=[d_],
                          bias=nacum.ap()[:, h:h + 1])
                P.tt("dve", m_.ap(), d_.ap(), sc.ap()[:, g:g + 1, :].to_broadcast([128, 4, 128]), ALU.mult,
                     reads=[d_, sc], writes=[m_])

        def stA_y(c, g):
            ps_y = ps_yl[g % 2]
            for hb_ in range(2):
                m_ = Mb[hb_]
                for hh in range(4):
                    hl = hb_ * 4 + hh
                    P.mm(ps_y.ap()[:, hl * 64:(hl + 1) * 64], m_.ap()[:, hh, :], xdtg[g].ap()[:, hl * 64:(hl + 1) * 64], True, True,
                         reads=[m_, xdtg[g]], writes=[ps_y])

        def stA2(c, g):
            adt, acum, nacum, eacum, dte, cdec = (adt2[c % 2], acum2[c % 2], nacum2[c % 2], eacum2[c % 2], dte2[c % 2], cdec2[c % 2])
            acT_hi, acT_lo, sc = acT_hi2[c % 2], acT_lo2[c % 2], sc2[c % 2]
            b = c % 2
            Bt_, CT_ = Bt_c[b], CT_c[b]
            ps_y, ps_o = ps_yl[g % 2], ps_ol[g % 2]
            P.mm(ps_o.ap(), CT_.ap()[:, g, :], state_b[g].ap(), True, True, reads=[CT_, state_b[g]], writes=[ps_o])
            yo, yg_ = yoffs[g % 2], ygb[g % 3]
            P.tt("dve", v3(yo.ap()), v3(ps_o.ap()), bc(eacum.ap()[:, g * 8:(g + 1) * 8], 64), ALU.mult,
                 reads=[ps_o, eacum], writes=[yo])
            P.tt("dve", yg_.ap(), ps_y.ap(), yo.ap(), ALU.add, reads=[ps_y, yo], writes=[yg_])
            P.mm(ps_st.ap(), Bt_.ap()[:, g * 128:(g + 1) * 128], xdteg[g].ap(), True, True,
                 reads=[Bt_, xdteg[g]], writes=[ps_st])
            P.tt("dve", v3(state_f[g].ap()), v3(state_f[g].ap()), bc(cdec.ap()[:, g * 8:(g + 1) * 8], 64),
                 ALU.mult, reads=[state_f[g], cdec], writes=[state_f[g]])
            P.tt("dve", state_f[g].ap(), state_f[g].ap(), ps_st.ap(), ALU.add, reads=[state_f[g], ps_st],
                 writes=[state_f[g]])
            P.cp("act", state_b[g].ap(), state_f[g].ap(), reads=[state_f[g]], writes=[state_b[g]])

        def stB(c, g):
            zs_ = zs_c[c % 2]
            gs = slice(g * 512, (g + 1) * 512)
            yg_, sq_, ss_, yn_ = ygb[g % 3], sqg[g % 2], ssqb[g % 3], ynb[g % 3]
            P.tt("pool", yg_.ap(), yg_.ap(), dxg[g].ap(), ALU.add, reads=[yg_, dxg[g]], writes=[yg_])
            P.tt("dve", yg_.ap(), yg_.ap(), zs_.ap()[:, gs], ALU.mult, reads=[yg_, zs_], writes=[yg_])
            P.tt("dve", sq_.ap(), yg_.ap(), yg_.ap(), ALU.mult, reads=[yg_], writes=[sq_])
            P.op("dve", lambda e, a=ss_, q=sq_: e.reduce_sum(a.ap(), q.ap(), mybir.AxisListType.X), reads=[sq_], writes=[ss_])
            P.act(ss_.ap(), ss_.ap(), AF.Ln, reads=[ss_], writes=[ss_], scale=1.0 / 512, bias=GEPS)
            P.act(ss_.ap(), ss_.ap(), AF.Exp, reads=[ss_], writes=[ss_], scale=-0.5)
            P.stt("dve", yn_.ap(), yg_.ap(), ss_.ap()[:, 0:1], gnw_t.ap()[:, gs], ALU.mult, ALU.mult,
                  reads=[yg_, ss_, gnw_t], writes=[yn_])

        def stC(c, g):
            yn_ = ynb[g % 3]
            po = (g % 2) * 512
            for i in range(4):
                P.tr(pst.ap()[:, po + i * 128:po + (i + 1) * 128], yn_.ap()[:, i * 128:(i + 1) * 128], ident_bf.ap(),
                     reads=[yn_, ident_bf], writes=[pst])
            P.cp("act", yts.ap()[:, g * 4:(g + 1) * 4, :], pst.ap()[:, po:po + 512].rearrange("p (a b) -> p a b", a=4),
                 reads=[pst], acc=[yts])

        load_chunk(0)
        prologue(0)
        stP(0, 0)
        stP(0, 1)
        stP(0, 2)
        for c in range(NB):
            if c + 1 < NB:
                load_chunk(c + 1)
            stA_seg(c, 0)
            stA_exp(c, 0)
            stA_y(c, 0)
            for g in range(8):
                if g + 3 < 8:
                    stP(c, g + 3)
                if g + 1 < 8:
                    stA_seg(c, g + 1)
                    stA_exp(c, g + 1)
                stA2(c, g)
                if g >= 2:
                    stC(c, g - 2)
                if g + 1 < 8:
                    stA_y(c, g + 1)
                if g >= 1:
                    stB(c, g - 1)
                if c + 1 < NB:
                    if g == 4:
                        prologue(c + 1)
                    if g >= 5:
                        stP(c + 1, g - 5)
            stB(c, 7)
            stC(c, 6)
            stC(c, 7)
            for k2 in range(2):
                P.dma("sp", yT.rearrange("(kc p) t -> p kc t", p=128)[:, k2 * 16:(k2 + 1) * 16, c * 128:(c + 1) * 128],
                      yts.ap()[:, k2 * 16:(k2 + 1) * 16, :], reads=[yts], acc=[yTR])

        P.stage_begin(base=U_OFF)
        YT = P.sb("YT", [128, 32 * S], BF16)
        Yv = YT.ap().rearrange("p (k t) -> p k t", k=32)
        for k8 in range(8):
            P.dma("sp", Yv[:, k8 * 4:(k8 + 1) * 4, :],
                  yT.rearrange("(kc p) t -> p kc t", p=128)[:, k8 * 4:(k8 + 1) * 4, :], reads=[yTR], acc=[YT])
        pre, evac = resid_hooks(h_in, hinR, h_out, houtR, 0)
        proj(Yv, YT, 32, S, ssd_out_w[j], D, "F", evac, pre=pre)

    def ple(i, h_in, hinR, h_out, houtR):
        norm_stage(h_in, hinR, plenw_sb, i * 16)
        P.stage_begin()
        pw_b = P.sb("pw_b", [128, 2, D], BF16)
        pT_b = P.sb("pT_b", [128, 2, S], BF16)
        P.dma("pool", pw_b.ap(), pproj_w[i].rearrange("(kc p) f -> p kc f", p=128), reads=[win], writes=[pw_b])
        P.dma("pool", pT_b.ap(), pT[i].rearrange("(kc p) t -> p kc t", p=128), reads=[win], writes=[pT_b])
        gb = [P.sb("gb", [128, 512], F32) for _ in range(2)]
        gk = [0]

        def extra(f0, tt, ps, TW, hbuf):
            G = gb[gk[0] % 2]
            ps2 = psb[4 + gk[0] % 2]
            gk[0] += 1
            P.act(G.ap()[:, :TW], ps.ap()[:, :TW], AF.Sigmoid, reads=[ps], writes=[G])
            for kc in range(2):
                P.mm(ps2.ap()[:, :TW], pw_b.ap()[:, kc, f0:f0 + 128], pT_b.ap()[:, kc, tt * TW:(tt + 1) * TW], kc == 0, kc == 1,
                     reads=[pw_b, pT_b], writes=[ps2])
            P.tt("dve", G.ap()[:, :TW], ps2.ap()[:, :TW], G.ap()[:, :TW], ALU.mult, reads=[ps2, G], writes=[G])
            P.tt("pool", hbuf.ap()[:, :TW], hbuf.ap()[:, :TW], G.ap()[:, :TW], ALU.add, reads=[hbuf, G], writes=[hbuf])

        pre, evac = resid_hooks(h_in, hinR, h_out, houtR, 0, extra=extra)
        proj(Uv16, U, 16, S, gate_w[i], D, "F", evac, pre=pre)

    def sb_layer(j, h_in, hinR, h_out, houtR, nofs):
        norm_stage(h_in, hinR, normw_sb, nofs)
        P.stage_begin()
        w = sb_in_w[j]
        sqb = [P.sb("sqb", [128, 512], BF16) for _ in range(2)]
        rb = [P.sb("rb", [128, 512], F32) for _ in range(2)]
        qraw = [P.sb("qraw", [128, 512], F32) for _ in range(2)]
        qst = [P.sb("qst", [128, S], BF16) for _ in range(2)]
        vst = [P.sb("vst", [128, NB, 256], BF16) for _ in range(2)]
        qk = [0]

        def mk_evac_qk(dst, dstR, wsb):
            def ev(f0, tt, ps, TW):
                k = qk[0] % 2
                qk[0] += 1
                P.act(sqb[k].ap()[:, :TW], ps.ap()[:, :TW], AF.Square, reads=[ps], writes=[sqb[k]])
                ps2 = psb[4 + k]
                P.mm(ps2.ap()[:, :TW], ones_bf.ap(), sqb[k].ap()[:, :TW], True, True, reads=[ones_bf, sqb[k]], writes=[ps2])
                P.act(rb[k].ap()[:, :TW], ps2.ap()[:, :TW], AF.Ln, reads=[ps2], writes=[rb[k]], scale=1.0 / 128, bias=EPS)
                P.act(rb[k].ap()[:, :TW], rb[k].ap()[:, :TW], AF.Exp, reads=[rb[k]], writes=[rb[k]], scale=-0.5)
                qs = qst[(f0 // 128) % 2]
                P.cp("act", qraw[k].ap()[:, :TW], ps.ap()[:, :TW], reads=[ps], writes=[qraw[k]])
                P.stt("dve", qs.ap()[:, tt * TW:(tt + 1) * TW], qraw[k].ap()[:, :TW], wsb.ap()[:, j:j + 1], rb[k].ap()[:, :TW],
                      ALU.mult, ALU.mult, reads=[qraw[k], rb[k], wsb], acc=[qs])
                if tt == S // TW - 1:
                    P.dma("sp", dst[f0:f0 + 128, :], qs.ap(), reads=[qs], acc=[dstR])
            return ev

        proj(Uv16, U, 16, S, w[:, 0:D], D, "F", mk_evac_qk(qT, qR, qnw_sb))
        proj(Uv16, U, 16, S, w[:, D:2 * D], D, "F", mk_evac_qk(kT, kR, knw_sb))

        def evac_v(f0, tb, ps, FB):
            buf = vst[(f0 // 256) % 2]
            P.cp("act", buf.ap()[:, tb, :], ps.ap()[:, :256], reads=[ps], acc=[buf])
            if tb == NB - 1:
                P.dma("sp", v_tok.rearrange("(tb p) f -> p tb f", p=128)[:, :, f0:f0 + 256], buf.ap(), reads=[buf], acc=[vR])

        proj(Uv16, U, 16, S, w[:, 2 * D:3 * D], D, "T", evac_v)

        def evac_g(f0, tt, ps, TW):
            qs = qst[(f0 // 128) % 2]
            P.act(qs.ap()[:, tt * TW:(tt + 1) * TW], ps.ap()[:, :TW], AF.Silu, reads=[ps], acc=[qs])
            if tt == S // TW - 1:
                P.dma("sp", sgT[f0:f0 + 128, :], qs.ap(), reads=[qs], acc=[sgR])

        proj(Uv16, U, 16, S, w[:, 3 * D:4 * D], D, "F", evac_g)

        P.stage_begin()
        NH = 4
        qh = [P.sb("qh", [128, S], BF16) for _ in range(NH)]
        kh = [P.sb("kh", [128, S], BF16) for _ in range(NH)]
        nqh = [P.sb("nqh", [128, S], BF16) for _ in range(NH)]
        vh = [P.sb("vh", [128, NB, 128], BF16) for _ in range(NH)]
        sgh = [P.sb("sgh", [128, S], BF16) for _ in range(NH)]
        NSL = 4

        class _View(Res):
            def ap(self):
                return self.h[:].bitcast(F32)

        pst_f = _View("pst_f", pst.h)
        R_f = [P.sb("R_f", [128, 512], F32) for _ in range(NSL)]
        R_b = [[P.sb("R_b", [128, 512], BF16) for _ in range(2)] for _ in range(NSL)]
        Eb = [P.sb("Eb", [128, 512], F32) for _ in range(NSL)]
        Lb = [P.sb("Lb", [128, 512], BF16) for _ in range(NSL)]
        attb = [P.sb("attb", [128, 512], BF16) for _ in range(NSL)]
        ogb = [P.sb("ogb", [128, 512], BF16) for _ in range(NSL)]
        Zp = [psb[0], psb[2], psb[4], psb[6]]
        Pp = Zp
        Op_ = [psb[1], psb[3], psb[5], pst_f]

        def load_head(h):
            b = h % NH
            r = slice(h * 128, (h + 1) * 128)
            P.dma("sp", qh[b].ap(), qT[r, :], reads=[qR], writes=[qh[b]])
            P.dma("sp", kh[b].ap(), kT[r, :], reads=[kR], writes=[kh[b]])
            P.dma("sp", sgh[b].ap(), sgT[r, :], reads=[sgR], writes=[sgh[b]])
            P.dma("sp", vh[b].ap(), v_tok.rearrange("(tb p) f -> p tb f", p=128)[:, :, r], reads=[vR], writes=[vh[b]])
            P.ts("pool", nqh[b].ap(), qh[b].ap(), -1.0, 0.0, ALU.mult, ALU.add, reads=[qh[b]], writes=[nqh[b]])

        def stream(sl, h, qc):
            b = h % NH
            qs_ = slice(qc * 512, (qc + 1) * 512)
            E, L, att, Z, Pq, O, Rf = Eb[sl], Lb[sl], attb[sl], Zp[sl], Pp[sl], Op_[sl], R_f[sl]
            first = True
            n = 0
            for sb_ in range(min(4 * qc + 3, NB - 1), -1, -1):
                r = sb_ - 4 * qc
                ks = slice(sb_ * 128, (sb_ + 1) * 128)
                P.mm(Z.ap(), kh[b].ap()[:, ks], qh[b].ap()[:, qs_], True, True, reads=[kh[b], qh[b]], writes=[Z])
                yield
                c0 = max(r, 0) * 128
                P.act(E.ap()[:, c0:], Z.ap()[:, c0:], AF.Exp, reads=[Z], writes=[E])
                P.act(L.ap()[:, c0:], E.ap()[:, c0:], AF.Ln, reads=[E], writes=[L], bias=1.0)
                if r >= 0:
                    if c0 > 0:
                        P.memset("pool", L.ap()[:, :c0], 0.0, acc=[L])
                    P.tt("pool", L.ap()[:, c0:c0 + 128], L.ap()[:, c0:c0 + 128], maskr[r].ap()[:, c0:c0 + 128], ALU.mult,
                         reads=[L, maskr[r]], writes=[L])
                yield
                P.mm(Pq.ap(), tincl_bf.ap(), L.ap(), True, False, reads=[tincl_bf, L], writes=[Pq])
                if not first:
                    rb_ = R_b[sl][n % 2]
                    P.mm(Pq.ap(), ones_bf.ap(), rb_.ap(), False, False, reads=[ones_bf, rb_], writes=[Pq])
                P.mm(Pq.ap(), kh[b].ap()[:, ks], nqh[b].ap()[:, qs_], False, True, reads=[kh[b], nqh[b]], writes=[Pq])
                yield
                P.act(att.ap()[:, c0:], Pq.ap()[:, c0:], AF.Exp, reads=[Pq], writes=[att], scale=-1.0)
                if r >= 0:
                    if c0 > 0:
                        P.memset("pool", att.ap()[:, :c0], 0.0, acc=[att])
                    P.tt("pool", att.ap()[:, c0:c0 + 128], att.ap()[:, c0:c0 + 128], maskr[r].ap()[:, c0:c0 + 128], ALU.mult,
                         reads=[att, maskr[r]], writes=[att])
                if sb_ > 0:
                    if first:
                        P.cp("dve", Rf.ap(), L.ap(), reads=[L], writes=[Rf])
                    else:
                        P.tt("dve", Rf.ap(), Rf.ap(), L.ap(), ALU.add, reads=[Rf, L], writes=[Rf])
                    n += 1
                    P.cp("dve", R_b[sl][n % 2].ap(), Rf.ap(), reads=[Rf], writes=[R_b[sl][n % 2]])
                yield
                P.mm(O.ap(), vh[b].ap()[:, sb_, :], att.ap(), first, sb_ == 0, reads=[vh[b], att], writes=[O])
                first = False
            og = ogb[sl]
            P.tt("dve", og.ap(), O.ap(), sgh[b].ap()[:, qs_], ALU.mult, reads=[O, sgh[b]], writes=[og])
            P.dma("sp", yT[h * 128:(h + 1) * 128, qs_], og.ap(), reads=[og], acc=[yTR])

        tasks = [(h, qc) for h in range(16) for qc in range(S // 512 - 1, -1, -1)]
        load_head(0)
        loaded = 0
        active = [None] * NSL
        ti = 0
        while True:
            for sl in range(NSL):
                if active[sl] is None and ti < len(tasks):
                    h, qc = tasks[ti]
                    ti += 1
                    if h + 1 < 16 and loaded < h + 1:
                        load_head(h + 1)
                        loaded = h + 1
                    active[sl] = stream(sl, h, qc)
            if all(a is None for a in active):
                break
            for sl in range(NSL):
                if active[sl] is not None:
                    try:
                        next(active[sl])
                    except StopIteration:
                        active[sl] = None

        P.stage_begin()
        for k4 in range(4):
            P.dma("sp", Uv16[:, k4 * 4:(k4 + 1) * 4, :],
                  yT.rearrange("(kc p) t -> p kc t", p=128)[:, k4 * 4:(k4 + 1) * 4, :], reads=[yTR], acc=[U])
        pre, evac = resid_hooks(h_in, hinR, h_out, houtR, 0)
        proj(Uv16, U, 16, S, sb_out_w[j], D, "F", evac, pre=pre)

    cur, curR = xT, win
    try:
        for n, i in enumerate(layers):
            last = n == len(layers) - 1
            if i % 2 == 0:
                ssd_layer(i // 2, cur, curR, hB, hBR, i * 16)
            else:
                sb_layer(i // 2, cur, curR, hB, hBR, i * 16)
            dst, dstR = (out, outR) if last else (hA, hAR)
            ple(i, hB, hBR, dst, dstR)
            cur, curR = hA, hAR
    except StopBuild:
        pass
    P.barrier()
    P.wait("sp", [outR])
    P.emit()
    return nc, P


def prep_shared(inp):
    f = lambda a: np.ascontiguousarray(np.asarray(a, dtype=np.float32))
    rep = lambda a: f(np.broadcast_to(np.asarray(a, dtype=np.float32).reshape(1, -1), (128, np.asarray(a).size)))
    return {
        "normw": f(np.asarray(inp["norm_w"]).reshape(4, 16, 128).transpose(2, 0, 1).reshape(128, 64)),
        "plenw": f(np.asarray(inp["ple_norm_w"]).reshape(4, 16, 128).transpose(2, 0, 1).reshape(128, 64)),
        "convw": f(np.asarray(inp["ssd_conv_w"]).reshape(2, 4, 48, 128).transpose(3, 0, 2, 1).reshape(128, 384)),
        "convb": f(np.asarray(inp["ssd_conv_b"]).reshape(2, 48, 128).transpose(2, 0, 1).reshape(128, 96)),
        "dtb": rep(inp["ssd_dt_bias"]),
        "alog": rep(inp["ssd_a_log"]),
        "dskip": rep(inp["ssd_d"]),
        "gnw": rep(inp["ssd_gnorm_w"]),
        "qnw": f(np.asarray(inp["sb_qn_w"]).T),
        "knw": f(np.asarray(inp["sb_kn_w"]).T),
        "ssd_in_w": f(inp["ssd_in_w"]),
        "ssd_out_w": f(inp["ssd_out_w"]),
        "sb_in_w": f(inp["sb_in_w"]),
        "sb_out_w": f(inp["sb_out_w"]),
        "gate_w": f(inp["ple_gate_w"]),
        "pproj_w": f(inp["ple_proj_w"]),
    }


def prep_core(inp, b, shared):
    m = dict(shared)
    m["xT"] = np.ascontiguousarray(np.asarray(inp["x"][b], dtype=np.float32).T)
    m["pT"] = np.ascontiguousarray(np.asarray(inp["p"], dtype=np.float32)[:, b].transpose(0, 2, 1))
    return m


_NC_CACHE = {}


def kernel(**inputs):
    x = np.asarray(inputs["x"])
    B, S, _ = x.shape
    key = (S,)
    if key not in _NC_CACHE:
        _NC_CACHE[key] = build(S=S)[0]
    nc = _NC_CACHE[key]
    shared = prep_shared(inputs)
    in_maps = [prep_core(inputs, b, shared) for b in range(B)]
    res = run_bass_kernel_spmd(nc, in_maps, core_ids=list(range(B)))
    outs = [np.asarray(r["out"]).T for r in res.results]
    return np.ascontiguousarray(np.stack(outs, axis=0).astype(np.float32))
```

```python
import numpy as np
import concourse.bass as bass
import concourse.mybir as mybir
from concourse.bass_utils import run_bass_kernel_spmd

F32 = mybir.dt.float32
BF16 = mybir.dt.bfloat16
AF = mybir.ActivationFunctionType
ALU = mybir.AluOpType

EPOCH = 12000


class Res:
    __slots__ = ("name", "writers", "readers", "dsem", "dcount", "h")

    def __init__(self, name, h=None):
        self.name = name
        self.writers = []
        self.readers = []
        self.dsem = None
        self.dcount = 0
        self.h = h

    def ap(self):
        return self.h[:]


class Op:
    __slots__ = ("eng", "fn", "deps", "signal", "is_dma", "sem", "val", "key")

    def __init__(self, eng, fn, deps, is_dma):
        self.eng = eng
        self.fn = fn
        self.deps = deps
        self.signal = is_dma
        self.is_dma = is_dma
        self.sem = None
        self.val = 0
        self.key = eng


def _merge(lst, op):
    for i, o in enumerate(lst):
        if o.key == op.key:
            lst[i] = op
            return
    lst.append(op)


SB_LO = 16512
SB_HI = 229344


class StopBuild(Exception):
    pass


class Prog:
    ENGS = ("pe", "act", "dve", "pool", "sp")

    def __init__(self, nc):
        self.nc = nc
        self.streams = {e: [] for e in self.ENGS}
        self.nsem = 0
        self.nres = 0
        self.persist_ptr = SB_LO
        self.stage_base = SB_LO
        self.stage_ptr = SB_LO
        self.stage_res = []
        self.sem_pool = []
        self.last_dma = {}
        self.last_cmp = {}
        self.wcache = {}

    def _alloc(self, name, shape, dtype, off):
        self.nres += 1
        return self.nc.alloc_sbuf_tensor_at(f"{name}_{self.nres}", list(shape), dtype, offset=off)

    @staticmethod
    def _bytes(shape, dtype):
        n = 1
        for s in shape[1:]:
            n *= s
        n *= 2 if dtype == BF16 else 4
        return (n + 63) // 64 * 64

    def sbp(self, name, shape, dtype):
        off = self.persist_ptr
        self.persist_ptr += self._bytes(shape, dtype)
        assert self.persist_ptr <= SB_HI
        self.stage_base = self.persist_ptr
        self.stage_ptr = self.persist_ptr
        return Res(name, self._alloc(name, shape, dtype, off))

    def sb(self, name, shape, dtype):
        off = self.stage_ptr
        self.stage_ptr += self._bytes(shape, dtype)
        assert self.stage_ptr <= SB_HI, f"SBUF overflow at {name}: {self.stage_ptr}"
        r = Res(name, self._alloc(name, shape, dtype, off))
        self.stage_res.append(r)
        return r

    def ps(self, name, shape, dtype=F32):
        self.nres += 1
        return Res(name, self.nc.alloc_psum_tensor(f"{name}_{self.nres}", list(shape), dtype))

    def new_sem(self, name):
        self.nsem += 1
        return self.nc.alloc_semaphore(name=f"{name}_{self.nsem}")

    def stage_begin(self, base=None):
        self.nstage = getattr(self, "nstage", 0) + 1
        if self.nstage > getattr(self, "max_stages", 10 ** 9):
            raise StopBuild()
        self.barrier()
        for r in self.stage_res:
            if r.dsem is not None:
                self.sem_pool.append((r.dsem, r.dcount))
                r.dsem = None
        self.stage_res = []
        self.wcache = {}
        self.stage_ptr = self.stage_base if base is None else base

    def barrier(self):
        deps = list(self.last_cmp.values()) + list(self.last_dma.values())
        if not deps:
            return
        for e in self.ENGS:
            self.streams[e].append(Op(e, None, list(deps), False))

    def _deps(self, eng, reads, writes, acc):
        deps = []
        seen = set()

        def add(o):
            if id(o) in seen:
                return
            if o.eng == "pe" and eng == "pe" and not o.is_dma:
                return
            seen.add(id(o))
            deps.append(o)

        for r in reads:
            for o in r.writers:
                add(o)
        for w in writes:
            for o in w.writers:
                add(o)
            for o in w.readers:
                add(o)
        for w in acc:
            if w.readers:
                for o in w.writers:
                    add(o)
                for o in w.readers:
                    add(o)
        return deps

    def _commit(self, op, reads, writes, acc):
        for w in acc:
            if w.readers:
                w.writers = [op]
                w.readers = []
            else:
                _merge(w.writers, op)
        for w in writes:
            w.writers = [op]
            w.readers = []
        for r in reads:
            _merge(r.readers, op)

    def op(self, eng, fn, reads=(), writes=(), acc=()):
        o = Op(eng, fn, self._deps(eng, reads, writes, acc), False)
        self.streams[eng].append(o)
        self._commit(o, reads, writes, acc)
        self.last_cmp[eng] = o
        return o

    def dma(self, eng, out_ap, in_ap, reads=(), writes=(), acc=()):
        dst = (list(writes) + list(acc))[0]
        if dst.dsem is None:
            if self.sem_pool:
                dst.dsem, dst.dcount = self.sem_pool.pop()
            else:
                dst.dsem, dst.dcount = self.new_sem("d"), 0
        o = Op(eng, None, self._deps(eng, reads, writes, acc), True)
        dst.dcount += 16
        o.sem = dst.dsem
        o.val = dst.dcount
        o.key = ("d", id(dst.dsem))
        o.fn = lambda e, a=out_ap, b=in_ap: e.dma_start(out=a, in_=b)
        self.streams[eng].append(o)
        self._commit(o, reads, writes, acc)
        self.last_dma[id(dst.dsem)] = o
        return o

    def wait(self, eng, ress):
        deps = []
        for r in ress:
            deps += r.writers
        o = Op(eng, None, deps, False)
        self.streams[eng].append(o)
        return o

    def mm(self, out, lhsT, rhs, start, stop, reads, writes):
        return self.op("pe", lambda e: e.matmul(out, lhsT, rhs, start=start, stop=stop), reads, writes)

    def tr(self, out, in_, ident, reads, writes):
        return self.op("pe", lambda e: e.transpose(out, in_, ident), reads, writes)

    def act(self, out, in_, func, reads=(), writes=(), acc=(), **kw):
        return self.op("act", lambda e: e.activation(out, in_, func, **kw), reads, writes, acc)

    def tt(self, eng, out, a, b, op, reads=(), writes=(), acc=()):
        return self.op(eng, lambda e: e.tensor_tensor(out, a, b, op), reads, writes, acc)

    def stt(self, eng, out, in0, scalar, in1, op0, op1, reads=(), writes=(), acc=()):
        return self.op(eng, lambda e: e.scalar_tensor_tensor(out, in0, scalar, in1, op0, op1), reads, writes, acc)

    def ts(self, eng, out, in0, s1, s2, op0, op1, reads=(), writes=(), acc=()):
        return self.op(eng, lambda e: e.tensor_scalar(out, in0, s1, s2, op0, op1), reads, writes, acc)

    def cp(self, eng, out, in_, reads=(), writes=(), acc=()):
        if eng == "act":
            return self.op(eng, lambda e: e.activation(out, in_, AF.Copy), reads, writes, acc)
        return self.op(eng, lambda e: e.tensor_copy(out, in_), reads, writes, acc)

    def memset(self, eng, out, val, writes=(), acc=()):
        return self.op(eng, lambda e: e.memset(out, val), (), writes, acc)

    def emit(self):
        nc = self.nc
        for e in self.ENGS:
            for o in self.streams[e]:
                for d in o.deps:
                    d.signal = True
        for e in self.ENGS:
            cnt = 0
            sem = None
            for o in self.streams[e]:
                if o.is_dma or not o.signal or o.fn is None:
                    continue
                if sem is None or cnt >= EPOCH:
                    sem = self.new_sem("c_" + e)
                    cnt = 0
                cnt += 1
                o.sem = sem
                o.val = cnt

        def run(ename, eng):
            waited = {}
            for o in self.streams[ename]:
                for d in o.deps:
                    k = id(d.sem)
                    if waited.get(k, 0) >= d.val:
                        continue
                    eng.wait_ge(d.sem, d.val)
                    waited[k] = d.val
                if o.fn is None:
                    continue
                ins = o.fn(eng)
                if o.signal:
                    ins.then_inc(o.sem, 16 if o.is_dma else 1)

        with nc.Block() as block:
            @block.tensor
            def _(eng):
                run("pe", eng)

            @block.scalar
            def _(eng):
                run("act", eng)

            @block.vector
            def _(eng):
                run("dve", eng)

            @block.gpsimd
            def _(eng):
                run("pool", eng)

            @block.sync
            def _(eng):
                run("sp", eng)


import os
SCAN_CUT = float(os.environ.get("SCAN_CUT", "9"))
D = 2048
KC = 16
EPS = 1e-6
GEPS = 1e-5
NEG = -30000.0


def build(S=2048, layers=(0, 1, 2, 3), dbg=False, max_stages=10 ** 9):
    NB = S // 128
    nc = bass.Bass("TRN2", target_bir_lowering=False)
    P = Prog(nc)
    P.max_stages = max_stages

    def din(name, shape, dt=F32):
        return nc.dram_tensor(name, list(shape), dt, kind="ExternalInput").ap()

    skind = "ExternalOutput" if dbg else "Internal"

    def dscr(name, shape, dt=BF16):
        return nc.dram_tensor(name, list(shape), dt, kind=skind).ap()

    xT = din("xT", [D, S])
    pT = din("pT", [4, 256, S])
    normw = din("normw", [128, 64])
    plenw = din("plenw", [128, 64])
    convw = din("convw", [128, 384])
    convb = din("convb", [128, 96])
    dtb = din("dtb", [128, 128])
    alog = din("alog", [128, 128])
    dskip = din("dskip", [128, 128])
    gnw = din("gnw", [128, 8192])
    qnw = din("qnw", [128, 2])
    knw = din("knw", [128, 2])
    ssd_in_w = din("ssd_in_w", [2, D, 10304])
    ssd_out_w = din("ssd_out_w", [2, 4096, D])
    sb_in_w = din("sb_in_w", [2, D, 8192])
    sb_out_w = din("sb_out_w", [2, D, D])
    gate_w = din("gate_w", [4, D, D])
    pproj_w = din("pproj_w", [4, 256, D])
    out = nc.dram_tensor("out", [D, S], F32, kind="ExternalOutput").ap()
    win = Res("win")
    outR = Res("out")

    hA = dscr("hA", [D, S], F32); hAR = Res("hA")
    hB = dscr("hB", [D, S], F32); hBR = Res("hB")
    zs_tok = dscr("zs_tok", [S, 4096]); zsR = Res("zs")
    xs_tok = dscr("xs_tok", [S, 4096]); xsR = Res("xs")
    B_tok = dscr("B_tok", [S, 1024]); BtR = Res("Bt")
    B_T = dscr("B_T", [1024, S]); BTR = Res("BT")
    C_T = dscr("C_T", [1024, S]); CTR = Res("CT")
    yT = dscr("yT", [4096, S]); yTR = Res("yT")
    qT = dscr("qT", [D, S]); qR = Res("q")
    kT = dscr("kT", [D, S]); kR = Res("k")
    v_tok = dscr("v_tok", [S, D]); vR = Res("v")
    sgT = dscr("sgT", [D, S]); sgR = Res("sg")

    psb = [P.ps(f"psb{i}", [128, 512], F32) for i in range(7)]
    pst = P.ps("pst", [128, 1024], BF16)

    ones_bf = P.sbp("ones_bf", [128, 128], BF16)
    ident_bf = P.sbp("ident_bf", [128, 128], BF16)
    tincl_bf = P.sbp("tincl_bf", [128, 128], BF16)
    ones_f = P.sbp("ones_f", [128, 128], F32)
    ident_f = P.sbp("ident_f", [128, 128], F32)
    tri_f = P.sbp("tri_f", [128, 128], F32)
    neg4 = P.sbp("neg4", [128, 512], F32)
    maskr = [P.sbp(f"mask{r}", [128, 512], BF16) for r in range(4)]
    normw_sb = P.sbp("normw_sb", [128, 64], F32)
    plenw_sb = P.sbp("plenw_sb", [128, 64], F32)
    convw_sb = P.sbp("convw_sb", [128, 384], F32)
    convb_sb = P.sbp("convb_sb", [128, 96], F32)
    dtb_sb = P.sbp("dtb_sb", [128, 128], F32)
    a_sb = P.sbp("a_sb", [128, 128], F32)
    dskip_sb = P.sbp("dskip_sb", [128, 128], F32)
    qnw_sb = P.sbp("qnw_sb", [128, 2], F32)
    knw_sb = P.sbp("knw_sb", [128, 2], F32)
    dtt = P.sbp("dtt", [128, NB, 64], F32)
    U_OFF = P.persist_ptr
    U = P.sbp("U", [128, 16 * S], BF16)

    def sel(res, pattern, op, fill, base, cm):
        P.op("pool", lambda e: e.affine_select(res.ap(), res.ap(), pattern=pattern, compare_op=op, fill=fill,
                                                base=base, channel_multiplier=cm), reads=[res], writes=[res])

    P.memset("pool", ones_bf.ap(), 1.0, writes=[ones_bf])
    P.memset("pool", ones_f.ap(), 1.0, writes=[ones_f])
    P.memset("pool", ident_bf.ap(), 1.0, writes=[ident_bf])
    sel(ident_bf, [[-1, 128]], ALU.is_equal, 0.0, 0, 1)
    P.memset("pool", ident_f.ap(), 1.0, writes=[ident_f])
    sel(ident_f, [[-1, 128]], ALU.is_equal, 0.0, 0, 1)
    P.memset("pool", tincl_bf.ap(), 1.0, writes=[tincl_bf])
    sel(tincl_bf, [[-1, 128]], ALU.is_ge, 0.0, 0, 1)
    P.memset("pool", tri_f.ap(), 1.0, writes=[tri_f])
    sel(tri_f, [[1, 128]], ALU.is_ge, 0.0, 0, -1)
    P.memset("pool", neg4.ap(), NEG, writes=[neg4])
    P.op("pool", lambda e: e.affine_select(neg4.ap().rearrange("p (a b) -> p a b", a=4),
                                            neg4.ap().rearrange("p (a b) -> p a b", a=4),
                                            pattern=[[0, 4], [-1, 128]], compare_op=ALU.is_gt, fill=0.0, base=0,
                                            channel_multiplier=1), reads=[neg4], writes=[neg4])
    for r in range(4):
        P.memset("pool", maskr[r].ap(), 1.0, writes=[maskr[r]])
        sel(maskr[r], [[1, 512]], ALU.is_gt, 0.0, -128 * r, -1)
    for sbt, dr in ((normw_sb, normw), (plenw_sb, plenw), (convw_sb, convw), (convb_sb, convb), (dtb_sb, dtb),
                    (a_sb, alog), (dskip_sb, dskip), (qnw_sb, qnw), (knw_sb, knw)):
        P.dma("sp", sbt.ap(), dr, reads=[win], writes=[sbt])
    P.act(a_sb.ap(), a_sb.ap(), AF.Exp, reads=[a_sb], writes=[a_sb])
    P.ts("dve", a_sb.ap(), a_sb.ap(), -1.0, 0.0, ALU.mult, ALU.add, reads=[a_sb], writes=[a_sb])
    P.ts("dve", qnw_sb.ap(), qnw_sb.ap(), float(128 ** -0.5), 0.0, ALU.mult, ALU.add, reads=[qnw_sb], writes=[qnw_sb])

    Uv16 = U.ap().rearrange("p (k t) -> p k t", k=16)

    def norm_stage(h_ap, h_res, wsb, wofs):
        P.stage_begin()
        TT = 256
        Hb = [P.sb("Hb", [128, 16, TT], F32) for _ in range(2)]
        sq = [P.sb("sq", [128, 16, TT], BF16) for _ in range(2)]
        rs = [P.sb("rs", [128, TT], F32) for _ in range(2)]
        hv = h_ap.rearrange("(kc p) t -> p kc t", p=128)
        for tt in range(S // TT):
            b = tt % 2
            sl = slice(tt * TT, (tt + 1) * TT)
            P.dma("sp", Hb[b].ap(), hv[:, :, sl], reads=[h_res], writes=[Hb[b]])
            P.act(sq[b].ap(), Hb[b].ap(), AF.Square, reads=[Hb[b]], writes=[sq[b]])
            ps = psb[b]
            for kc in range(16):
                P.mm(ps.ap()[:, :TT], ones_bf.ap(), sq[b].ap()[:, kc, :], kc == 0, kc == 15,
                     reads=[sq[b], ones_bf], writes=[ps])
            P.act(rs[b].ap(), ps.ap()[:, :TT], AF.Ln, reads=[ps], writes=[rs[b]], scale=1.0 / D, bias=EPS)
            P.act(rs[b].ap(), rs[b].ap(), AF.Exp, reads=[rs[b]], writes=[rs[b]], scale=-0.5)
            for kc in range(16):
                P.stt("dve", Uv16[:, kc, sl], Hb[b].ap()[:, kc, :], wsb.ap()[:, wofs + kc:wofs + kc + 1], rs[b].ap(),
                      ALU.mult, ALU.mult, reads=[Hb[b], wsb, rs[b]], acc=[U])

    pcnt = [0]

    def proj(Av, Ares, kc_n, T, w_ap, F, mode, evac, pre=None, FB=None):
        if FB is None:
            FB = min(F, 4096 // kc_n)
        key = kc_n * FB
        NWB = 3
        if key not in P.wcache:
            P.wcache[key] = [P.sb("wbf", [128, key], BF16) for _ in range(NWB)]
        wbf = P.wcache[key]
        wv = w_ap.rearrange("(kc p) f -> p kc f", p=128)
        nblk = F // FB
        TW = min(512, T)

        def load(i):
            b = i % NWB
            P.dma("pool", wbf[b].ap().rearrange("p (k f) -> p k f", k=kc_n), wv[:, :, i * FB:(i + 1) * FB],
                  reads=[win], writes=[wbf[b]])

        load(0)
        if nblk > 1:
            load(1)
        for i in range(nblk):
            if i + 2 < nblk:
                load(i + 2)
            wb = wbf[i % NWB]
            wbv = wb.ap().rearrange("p (k f) -> p k f", k=kc_n)
            f0 = i * FB
            if mode == "F":
                for sub in range(FB // 128):
                    for tt in range(T // TW):
                        ps = psb[pcnt[0] % 4]
                        pcnt[0] += 1
                        if pre is not None:
                            pre(f0 + sub * 128, tt, TW)
                        for kc in range(kc_n):
                            P.mm(ps.ap()[:, :TW], wbv[:, kc, sub * 128:(sub + 1) * 128], Av[:, kc, tt * TW:(tt + 1) * TW],
                                 kc == 0, kc == kc_n - 1, reads=[wb, Ares], writes=[ps])
                        evac(f0 + sub * 128, tt, ps, TW)
            else:
                for tb in range(T // 128):
                    ps = psb[pcnt[0] % 4]
                    pcnt[0] += 1
                    for kc in range(kc_n):
                        P.mm(ps.ap()[:, :FB], Av[:, kc, tb * 128:(tb + 1) * 128], wbv[:, kc, :],
                             kc == 0, kc == kc_n - 1, reads=[wb, Ares], writes=[ps])
                    evac(f0, tb, ps, FB)

    ecnt = [0]

    def resid_hooks(h_src, hsR, h_dst, hdR, t_off, extra=None):
        hb = [P.sb("hb", [128, 512], F32) for _ in range(4)]
        cur = {}

        def pre(f0, tt, TW):
            b = hb[ecnt[0] % 4]
            ecnt[0] += 1
            cur[(f0, tt)] = b
            t0 = t_off + tt * TW
            P.dma("act", b.ap()[:, :TW], h_src[f0:f0 + 128, t0:t0 + TW], reads=[hsR], writes=[b])

        def evac(f0, tt, ps, TW):
            b = cur.pop((f0, tt))
            t0 = t_off + tt * TW
            if extra is None:
                P.tt("dve", b.ap()[:, :TW], ps.ap()[:, :TW], b.ap()[:, :TW], ALU.add, reads=[ps, b], writes=[b])
            else:
                extra(f0, tt, ps, TW, b)
            P.dma("sp", h_dst[f0:f0 + 128, t0:t0 + TW], b.ap()[:, :TW], reads=[b], acc=[hdR])

        return pre, evac

    def transposes_to_tok(src, dst_ap, dstR, c0, xtrb, kidx):
        xtr = xtrb[kidx % 2]
        for g8 in range(NB // 8):
            for i in range(8):
                tb = g8 * 8 + i
                P.tr(pst.ap()[:, i * 128:(i + 1) * 128], src.ap()[:, tb * 128:(tb + 1) * 128], ident_bf.ap(),
                     reads=[src, ident_bf], writes=[pst])
            P.cp("dve", xtr.ap()[:, g8 * 8:(g8 + 1) * 8, :], pst.ap().rearrange("p (a b) -> p a b", a=8),
                 reads=[pst], acc=[xtr])
        P.dma("sp", dst_ap.rearrange("(tb p) c -> p tb c", p=128)[:, :, c0:c0 + 128], xtr.ap(), reads=[xtr], acc=[dstR])

    def ssd_layer(j, h_in, hinR, h_out, houtR, nofs):
        norm_stage(h_in, hinR, normw_sb, nofs)
        P.stage_begin()
        w = ssd_in_w[j]
        zst = [P.sb("zst", [128, NB, 256], BF16) for _ in range(2)]

        def evac_z(f0, tb, ps, FB):
            buf = zst[(f0 // 256) % 2]
            P.act(buf.ap()[:, tb, :], ps.ap()[:, :256], AF.Silu, reads=[ps], acc=[buf])
            if tb == NB - 1:
                P.dma("sp", zs_tok.rearrange("(tb p) f -> p tb f", p=128)[:, :, f0:f0 + 256], buf.ap(), reads=[buf], acc=[zsR])

        proj(Uv16, U, 16, S, w[:, 0:4096], 4096, "T", evac_z)

        P.stage_begin()
        xraw = [P.sb("xraw", [128, 4 + S], BF16) for _ in range(2)]
        xcb = [P.sb("xcb", [128, S], BF16) for _ in range(2)]
        xtrb = [P.sb("xtrb", [128, NB, 128], BF16) for _ in range(2)]
        dgb = [P.sb("dgb", [128, 4, 128], BF16) for _ in range(2)]
        for b in range(2):
            P.memset("pool", xraw[b].ap()[:, 0:4], 0.0, acc=[xraw[b]])
        cc = [0]
        pend = []

        def evac_c(f0, tt, ps, TW):
            blk = f0 // 128
            xr = xraw[blk % 2]
            xc = xcb[blk % 2]
            dg = dgb[blk % 2]
            wi = (j * 48 + blk) * 4
            if tt == 0:
                for k in range(4):
                    P.ts("pool", dg.ap()[:, k, :], ident_f.ap(), convw_sb.ap()[:, wi + k:wi + k + 1], 0.0, ALU.mult, ALU.add,
                         reads=[ident_f, convw_sb], acc=[dg])
            P.cp("act", xr.ap()[:, 4 + tt * TW:4 + (tt + 1) * TW], ps.ap()[:, :TW], reads=[ps], acc=[xr])
            while pend:
                pend.pop(0)()
            pend.append(lambda: conv_tile(blk, tt, TW))

        def conv_tile(blk, tt, TW):
            xr = xraw[blk % 2]
            xc = xcb[blk % 2]
            dg = dgb[blk % 2]
            ps2 = psb[4 + cc[0] % 2]
            cc[0] += 1
            for k in range(4):
                sh = 3 - k
                P.mm(ps2.ap()[:, :TW], dg.ap()[:, k, :], xr.ap()[:, 4 + tt * TW - sh:4 + (tt + 1) * TW - sh], k == 0, k == 3,
                     reads=[dg, xr], writes=[ps2])
            P.act(xc.ap()[:, tt * TW:(tt + 1) * TW], ps2.ap()[:, :TW], AF.Silu, reads=[ps2, convb_sb], acc=[xc],
                  bias=convb_sb.ap()[:, j * 48 + blk:j * 48 + blk + 1])
            if tt != S // TW - 1:
                return
            if blk >= 40:
                P.dma("sp", C_T[(blk - 40) * 128:(blk - 39) * 128, :], xc.ap(), reads=[xc], acc=[CTR])
            elif blk >= 32:
                P.dma("sp", B_T[(blk - 32) * 128:(blk - 31) * 128, :], xc.ap(), reads=[xc], acc=[BTR])
                transposes_to_tok(xc, B_tok, BtR, (blk - 32) * 128, xtrb, blk)
            else:
                transposes_to_tok(xc, xs_tok, xsR, blk * 128, xtrb, blk)

        proj(Uv16, U, 16, S, w[:, 4096:10240], 6144, "F", evac_c)
        while pend:
            pend.pop(0)()

        def evac_dt(f0, tb, ps, FB):
            P.tt("dve", dtt.ap()[:, tb, :], ps.ap()[:, :64], dtb_sb.ap()[:, j * 64:(j + 1) * 64], ALU.add,
                 reads=[ps, dtb_sb], acc=[dtt])

        proj(Uv16, U, 16, S, w[:, 10240:10304], 64, "T", evac_dt, FB=64)
        P.act(dtt.ap(), dtt.ap(), AF.Exp, reads=[dtt], writes=[dtt])
        P.act(dtt.ap(), dtt.ap(), AF.Ln, reads=[dtt], writes=[dtt], bias=1.0)

        P.stage_begin(base=U_OFF)
        gnw_t = P.sb("gnw_t", [128, 4096], BF16)
        P.dma("pool", gnw_t.ap(), gnw[:, j * 4096:(j + 1) * 4096], reads=[win], writes=[gnw_t])
        state_f = [P.sb("state_f", [128, 512], F32) for _ in range(8)]
        state_b = [P.sb("state_b", [128, 512], BF16) for _ in range(8)]
        for g in range(8):
            P.memset("pool", state_f[g].ap(), 0.0, writes=[state_f[g]])
            P.memset("pool", state_b[g].ap(), 0.0, writes=[state_b[g]])
        xs_c = [P.sb("xs_c", [128, 4096], BF16) for _ in range(2)]
        zs_c = [P.sb("zs_c", [128, 4096], BF16) for _ in range(2)]
        Bt_c = [P.sb("Bt_c", [128, 1024], BF16) for _ in range(2)]
        BT_c = [P.sb("BT_c", [128, 8, 128], BF16) for _ in range(2)]
        CT_c = [P.sb("CT_c", [128, 8, 128], BF16) for _ in range(2)]
        adt2 = [P.sb("adt", [128, 64], F32) for _ in range(2)]
        acum2 = [P.sb("acum", [128, 64], F32) for _ in range(2)]
        nacum2 = [P.sb("nacum", [128, 64], F32) for _ in range(2)]
        eacum2 = [P.sb("eacum", [128, 64], F32) for _ in range(2)]
        dte2 = [P.sb("dte", [128, 64], F32) for _ in range(2)]
        cdec2 = [P.sb("cdec", [128, 64], F32) for _ in range(2)]
        acT_hi2 = [P.sb("acT_hi", [64, 128], BF16) for _ in range(2)]
        acT_lo2 = [P.sb("acT_lo", [64, 128], BF16) for _ in range(2)]
        acT_t = P.sb("acT_t", [64, 128], F32)
        bdh = [P.sb("bdh", [64, 8, 128], BF16) for _ in range(4)]
        bdl = [P.sb("bdl", [64, 8, 128], BF16) for _ in range(4)]
        neg4b = P.sb("neg4b", [128, 512], BF16)
        P.cp("pool", neg4b.ap(), neg4.ap(), reads=[neg4], writes=[neg4b])
        sc2 = [P.sb("sc", [128, 8, 128], F32) for _ in range(2)]
        dec = [P.sb("dec", [128, 4, 128], F32) for _ in range(2)]
        Mb = [P.sb("Mb", [128, 4, 128], BF16) for _ in range(2)]
        xdtg = [P.sb("xdtg", [128, 512], BF16) for _ in range(8)]
        xdteg = [P.sb("xdteg", [128, 512], BF16) for _ in range(8)]
        dxg = [P.sb("dxg", [128, 512], BF16) for _ in range(8)]
        yoffs = [P.sb("yoffs", [128, 512], F32) for _ in range(2)]
        ygb = [P.sb("ygb", [128, 512], F32) for _ in range(3)]
        sqg = [P.sb("sqg", [128, 512], BF16) for _ in range(2)]
        ssqb = [P.sb("ssqb", [128, 1], F32) for _ in range(3)]
        ynb = [P.sb("ynb", [128, 512], BF16) for _ in range(3)]
        kkc = [0]
        yts = P.sb("yTst", [128, 32, 128], BF16)
        ps_a, ps_s, ps_st = psb[6], psb[1], psb[6]
        ps_seg = [psb[2], psb[3]]
        ps_yl = [psb[4], psb[0]]
        ps_ol = [psb[5], psb[1]]
        v3 = lambda ap_: ap_.rearrange("p (h d) -> p h d", d=64)
        bc = lambda ap_, n: ap_.unsqueeze(2).to_broadcast([128, ap_.shape[1], n])
        hs = slice(j * 64, (j + 1) * 64)

        def load_chunk(c):
            b = c % 2
            r = slice(c * 128, (c + 1) * 128)
            P.dma("sp", xs_c[b].ap(), xs_tok[r, :], reads=[xsR], writes=[xs_c[b]])
            P.dma("sp", Bt_c[b].ap(), B_tok[r, :], reads=[BtR], writes=[Bt_c[b]])
            P.dma("sp", BT_c[b].ap(), B_T.rearrange("(g n) t -> n g t", n=128)[:, :, r], reads=[BTR], writes=[BT_c[b]])
            P.dma("sp", CT_c[b].ap(), C_T.rearrange("(g n) t -> n g t", n=128)[:, :, r], reads=[CTR], writes=[CT_c[b]])
            P.dma("sp", zs_c[b].ap(), zs_tok[r, :], reads=[zsR], writes=[zs_c[b]])

        def prologue(c):
            adt, acum, nacum, eacum, dte, cdec = (adt2[c % 2], acum2[c % 2], nacum2[c % 2], eacum2[c % 2], dte2[c % 2], cdec2[c % 2])
            acT_hi, acT_lo, sc = acT_hi2[c % 2], acT_lo2[c % 2], sc2[c % 2]
            b = c % 2
            BT_, CT_ = BT_c[b], CT_c[b]
            P.tt("dve", adt.ap(), dtt.ap()[:, c, :], a_sb.ap()[:, hs], ALU.mult, reads=[dtt, a_sb], writes=[adt])
            P.mm(ps_a.ap()[:, 0:64], tri_f.ap(), adt.ap(), True, True, reads=[tri_f, adt], writes=[ps_a])
            P.mm(ps_a.ap()[:, 64:128], ones_f.ap(), adt.ap(), True, True, reads=[ones_f, adt], writes=[ps_a])
            P.cp("dve", acum.ap(), ps_a.ap()[:, 0:64], reads=[ps_a], writes=[acum])
            P.mm(ps_a.ap()[0:64, 128:256], acum.ap(), ident_f.ap(), True, True, reads=[acum, ident_f], writes=[ps_a])
            P.ts("dve", nacum.ap(), ps_a.ap()[:, 0:64], -1.0, 0.0, ALU.mult, ALU.add, reads=[ps_a], writes=[nacum])
            P.cp("dve", acT_hi.ap(), ps_a.ap()[0:64, 128:256], reads=[ps_a], writes=[acT_hi])
            P.tt("dve", acT_t.ap(), ps_a.ap()[0:64, 128:256], acT_hi.ap(), ALU.subtract, reads=[ps_a, acT_hi], writes=[acT_t])
            P.cp("dve", acT_lo.ap(), acT_t.ap(), reads=[acT_t], writes=[acT_lo])
            P.cp("dve", cdec.ap(), ps_a.ap()[:, 64:128], reads=[ps_a], writes=[cdec])
            P.tt("dve", dte.ap(), cdec.ap(), acum.ap(), ALU.subtract, reads=[cdec, acum], writes=[dte])
            P.act(eacum.ap(), acum.ap(), AF.Exp, reads=[acum], writes=[eacum])
            P.act(cdec.ap(), cdec.ap(), AF.Exp, reads=[cdec], writes=[cdec])
            P.act(dte.ap(), dte.ap(), AF.Exp, reads=[dte], writes=[dte])
            for g4 in range(2):
                for gi in range(4):
                    g = g4 * 4 + gi
                    P.mm(ps_s.ap()[:, gi * 128:(gi + 1) * 128], BT_.ap()[:, g, :], CT_.ap()[:, g, :], True, True,
                         reads=[BT_, CT_], writes=[ps_s])
                P.cp("dve", sc.ap()[:, g4 * 4:(g4 + 1) * 4, :], ps_s.ap().rearrange("p (a b) -> p a b", a=4),
                     reads=[ps_s], acc=[sc])

        def stP(c, g):
            adt, acum, nacum, eacum, dte, cdec = (adt2[c % 2], acum2[c % 2], nacum2[c % 2], eacum2[c % 2], dte2[c % 2], cdec2[c % 2])
            acT_hi, acT_lo, sc = acT_hi2[c % 2], acT_lo2[c % 2], sc2[c % 2]
            xs_ = xs_c[c % 2]
            gs = slice(g * 512, (g + 1) * 512)
            g8 = slice(g * 8, (g + 1) * 8)
            for src_, dst_ in ((acT_hi, bdh[g % 4]), (acT_lo, bdl[g % 4])):
                P.op("pool", lambda e, s_=src_, d_=dst_, g=g: e.affine_select(
                    d_.ap(), s_.ap().unsqueeze(1).to_broadcast([64, 8, 128]), pattern=[[-1, 8], [0, 128]],
                    compare_op=ALU.is_equal, fill=0.0, base=-8 * g, channel_multiplier=1), reads=[src_], writes=[dst_])
            P.tt("pool", v3(xdtg[g].ap()), v3(xs_.ap()[:, gs]), bc(dtt.ap()[:, c, g8], 64), ALU.mult,
                 reads=[xs_, dtt], writes=[xdtg[g]])
            P.tt("pool", v3(xdteg[g].ap()), v3(xdtg[g].ap()), bc(dte.ap()[:, g8], 64), ALU.mult,
                 reads=[xdtg[g], dte], writes=[xdteg[g]])
            P.tt("pool", v3(dxg[g].ap()), v3(xs_.ap()[:, gs]), bc(dskip_sb.ap()[:, j * 64 + g * 8:j * 64 + (g + 1) * 8], 64),
                 ALU.mult, reads=[xs_, dskip_sb], writes=[dxg[g]])

        def stA_seg(c, g):
            for hb_ in range(2):
                pseg = ps_seg[hb_]
                P.mm(pseg.ap(), ones_bf.ap()[0:64, :], bdh[g % 4].ap()[:, hb_ * 4:(hb_ + 1) * 4, :], True, False,
                     reads=[ones_bf, bdh[g % 4]], writes=[pseg])
                P.mm(pseg.ap(), ones_bf.ap()[0:64, :], bdl[g % 4].ap()[:, hb_ * 4:(hb_ + 1) * 4, :], False, False,
                     reads=[ones_bf, bdl[g % 4]], writes=[pseg])
                P.mm(pseg.ap(), ident_bf.ap(), neg4b.ap(), False, True, reads=[ident_bf, neg4b], writes=[pseg])

        def stA_exp(c, g):
            adt, acum, nacum, eacum, dte, cdec = (adt2[c % 2], acum2[c % 2], nacum2[c % 2], eacum2[c % 2], dte2[c % 2], cdec2[c % 2])
            acT_hi, acT_lo, sc = acT_hi2[c % 2], acT_lo2[c % 2], sc2[c % 2]
            for hb_ in range(2):
                pseg = ps_seg[hb_]
                h0 = g * 8 + hb_ * 4
                d_, m_ = dec[hb_], Mb[hb_]
                for hh in range(4):
                    h = h0 + hh
                    P.act(d_.ap()[:, hh, :], pseg.ap()[:, hh * 128:(hh + 1) * 128], AF.Exp, reads=[pseg, nacum], acc=[d_],
                          bias=nacum.ap()[:, h:h + 1])
                P.tt("dve", m_.ap(), d_.ap(), sc.ap()[:, g:g + 1, :].to_broadcast([128, 4, 128]), ALU.mult,
                     reads=[d_, sc], writes=[m_])

        def stA_y(c, g):
            ps_y = ps_yl[g % 2]
            for hb_ in range(2):
                m_ = Mb[hb_]
                for hh in range(4):
                    hl = hb_ * 4 + hh
                    P.mm(ps_y.ap()[:, hl * 64:(hl + 1) * 64], m_.ap()[:, hh, :], xdtg[g].ap()[:, hl * 64:(hl + 1) * 64], True, True,
                         reads=[m_, xdtg[g]], writes=[ps_y])

        def stA2(c, g):
            adt, acum, nacum, eacum, dte, cdec = (adt2[c % 2], acum2[c % 2], nacum2[c % 2], eacum2[c % 2], dte2[c % 2], cdec2[c % 2])
            acT_hi, acT_lo, sc = acT_hi2[c % 2], acT_lo2[c % 2], sc2[c % 2]
            b = c % 2
            Bt_, CT_ = Bt_c[b], CT_c[b]
            ps_y, ps_o = ps_yl[g % 2], ps_ol[g % 2]
            P.mm(ps_o.ap(), CT_.ap()[:, g, :], state_b[g].ap(), True, True, reads=[CT_, state_b[g]], writes=[ps_o])
            yo, yg_ = yoffs[g % 2], ygb[g % 3]
            P.tt("dve", v3(yo.ap()), v3(ps_o.ap()), bc(eacum.ap()[:, g * 8:(g + 1) * 8], 64), ALU.mult,
                 reads=[ps_o, eacum], writes=[yo])
            P.tt("dve", yg_.ap(), ps_y.ap(), yo.ap(), ALU.add, reads=[ps_y, yo], writes=[yg_])
            P.mm(ps_st.ap(), Bt_.ap()[:, g * 128:(g + 1) * 128], xdteg[g].ap(), True, True,
                 reads=[Bt_, xdteg[g]], writes=[ps_st])
            P.tt("dve", v3(state_f[g].ap()), v3(state_f[g].ap()), bc(cdec.ap()[:, g * 8:(g + 1) * 8], 64),
                 ALU.mult, reads=[state_f[g], cdec], writes=[state_f[g]])
            P.tt("dve", state_f[g].ap(), state_f[g].ap(), ps_st.ap(), ALU.add, reads=[state_f[g], ps_st],
                 writes=[state_f[g]])
            P.cp("act", state_b[g].ap(), state_f[g].ap(), reads=[state_f[g]], writes=[state_b[g]])

        def stB(c, g):
            zs_ = zs_c[c % 2]
            gs = slice(g * 512, (g + 1) * 512)
            yg_, sq_, ss_, yn_ = ygb[g % 3], sqg[g % 2], ssqb[g % 3], ynb[g % 3]
            P.tt("pool", yg_.ap(), yg_.ap(), dxg[g].ap(), ALU.add, reads=[yg_, dxg[g]], writes=[yg_])
            P.tt("dve", yg_.ap(), yg_.ap(), zs_.ap()[:, gs], ALU.mult, reads=[yg_, zs_], writes=[yg_])
            P.act(sq_.ap(), yg_.ap(), AF.Square, reads=[yg_], writes=[sq_])
            P.op("dve", lambda e, a=ss_, q=sq_: e.reduce_sum(a.ap(), q.ap(), mybir.AxisListType.X), reads=[sq_], writes=[ss_])
            P.act(ss_.ap(), ss_.ap(), AF.Ln, reads=[ss_], writes=[ss_], scale=1.0 / 512, bias=GEPS)
            P.act(ss_.ap(), ss_.ap(), AF.Exp, reads=[ss_], writes=[ss_], scale=-0.5)
            P.stt("dve", yn_.ap(), yg_.ap(), ss_.ap()[:, 0:1], gnw_t.ap()[:, gs], ALU.mult, ALU.mult,
                  reads=[yg_, ss_, gnw_t], writes=[yn_])

        def stC(c, g):
            yn_ = ynb[g % 3]
            po = (g % 2) * 512
            for i in range(4):
                P.tr(pst.ap()[:, po + i * 128:po + (i + 1) * 128], yn_.ap()[:, i * 128:(i + 1) * 128], ident_bf.ap(),
                     reads=[yn_, ident_bf], writes=[pst])
            P.cp("act", yts.ap()[:, g * 4:(g + 1) * 4, :], pst.ap()[:, po:po + 512].rearrange("p (a b) -> p a b", a=4),
                 reads=[pst], acc=[yts])

        load_chunk(0)
        prologue(0)
        stP(0, 0)
        stP(0, 1)
        stP(0, 2)
        for c in range(NB):
            if c + 1 < NB:
                load_chunk(c + 1)
            stA_seg(c, 0)
            stA_exp(c, 0)
            stA_y(c, 0)
            for g in range(8):
                if g + 3 < 8:
                    stP(c, g + 3)
                if g + 1 < 8:
                    stA_seg(c, g + 1)
                    stA_exp(c, g + 1)
                stA2(c, g)
                if g >= 2:
                    stC(c, g - 2)
                if g + 1 < 8:
                    stA_y(c, g + 1)
                if g >= 1:
                    stB(c, g - 1)
                if c + 1 < NB:
                    if g == 4:
                        prologue(c + 1)
                    if g >= 5:
                        stP(c + 1, g - 5)
            stB(c, 7)
            stC(c, 6)
            stC(c, 7)
            for k2 in range(2):
                P.dma("sp", yT.rearrange("(kc p) t -> p kc t", p=128)[:, k2 * 16:(k2 + 1) * 16, c * 128:(c + 1) * 128],
                      yts.ap()[:, k2 * 16:(k2 + 1) * 16, :], reads=[yts], acc=[yTR])

        P.stage_begin(base=U_OFF)
        YT = P.sb("YT", [128, 32 * S], BF16)
        Yv = YT.ap().rearrange("p (k t) -> p k t", k=32)
        for k8 in range(8):
            P.dma("sp", Yv[:, k8 * 4:(k8 + 1) * 4, :],
                  yT.rearrange("(kc p) t -> p kc t", p=128)[:, k8 * 4:(k8 + 1) * 4, :], reads=[yTR], acc=[YT])
        pre, evac = resid_hooks(h_in, hinR, h_out, houtR, 0)
        proj(Yv, YT, 32, S, ssd_out_w[j], D, "F", evac, pre=pre)

    def ple(i, h_in, hinR, h_out, houtR):
        norm_stage(h_in, hinR, plenw_sb, i * 16)
        P.stage_begin()
        pw_b = P.sb("pw_b", [128, 2, D], BF16)
        pT_b = P.sb("pT_b", [128, 2, S], BF16)
        P.dma("pool", pw_b.ap(), pproj_w[i].rearrange("(kc p) f -> p kc f", p=128), reads=[win], writes=[pw_b])
        P.dma("pool", pT_b.ap(), pT[i].rearrange("(kc p) t -> p kc t", p=128), reads=[win], writes=[pT_b])
        gb = [P.sb("gb", [128, 512], F32) for _ in range(2)]
        gk = [0]

        def extra(f0, tt, ps, TW, hbuf):
            G = gb[gk[0] % 2]
            ps2 = psb[4 + gk[0] % 2]
            gk[0] += 1
            P.act(G.ap()[:, :TW], ps.ap()[:, :TW], AF.Sigmoid, reads=[ps], writes=[G])
            for kc in range(2):
                P.mm(ps2.ap()[:, :TW], pw_b.ap()[:, kc, f0:f0 + 128], pT_b.ap()[:, kc, tt * TW:(tt + 1) * TW], kc == 0, kc == 1,
                     reads=[pw_b, pT_b], writes=[ps2])
            P.tt("dve", G.ap()[:, :TW], ps2.ap()[:, :TW], G.ap()[:, :TW], ALU.mult, reads=[ps2, G], writes=[G])
            P.tt("pool", hbuf.ap()[:, :TW], hbuf.ap()[:, :TW], G.ap()[:, :TW], ALU.add, reads=[hbuf, G], writes=[hbuf])

        pre, evac = resid_hooks(h_in, hinR, h_out, houtR, 0, extra=extra)
        proj(Uv16, U, 16, S, gate_w[i], D, "F", evac, pre=pre)

    def sb_layer(j, h_in, hinR, h_out, houtR, nofs):
        norm_stage(h_in, hinR, normw_sb, nofs)
        P.stage_begin()
        w = sb_in_w[j]
        sqb = [P.sb("sqb", [128, 512], BF16) for _ in range(2)]
        rb = [P.sb("rb", [128, 512], F32) for _ in range(2)]
        qraw = [P.sb("qraw", [128, 512], F32) for _ in range(2)]
        qst = [P.sb("qst", [128, S], BF16) for _ in range(2)]
        vst = [P.sb("vst", [128, NB, 256], BF16) for _ in range(2)]
        qk = [0]

        def mk_evac_qk(dst, dstR, wsb):
            def ev(f0, tt, ps, TW):
                k = qk[0] % 2
                qk[0] += 1
                P.act(sqb[k].ap()[:, :TW], ps.ap()[:, :TW], AF.Square, reads=[ps], writes=[sqb[k]])
                ps2 = psb[4 + k]
                P.mm(ps2.ap()[:, :TW], ones_bf.ap(), sqb[k].ap()[:, :TW], True, True, reads=[ones_bf, sqb[k]], writes=[ps2])
                P.act(rb[k].ap()[:, :TW], ps2.ap()[:, :TW], AF.Ln, reads=[ps2], writes=[rb[k]], scale=1.0 / 128, bias=EPS)
                P.act(rb[k].ap()[:, :TW], rb[k].ap()[:, :TW], AF.Exp, reads=[rb[k]], writes=[rb[k]], scale=-0.5)
                qs = qst[(f0 // 128) % 2]
                P.cp("act", qraw[k].ap()[:, :TW], ps.ap()[:, :TW], reads=[ps], writes=[qraw[k]])
                P.stt("dve", qs.ap()[:, tt * TW:(tt + 1) * TW], qraw[k].ap()[:, :TW], wsb.ap()[:, j:j + 1], rb[k].ap()[:, :TW],
                      ALU.mult, ALU.mult, reads=[qraw[k], rb[k], wsb], acc=[qs])
                if tt == S // TW - 1:
                    P.dma("sp", dst[f0:f0 + 128, :], qs.ap(), reads=[qs], acc=[dstR])
            return ev

        proj(Uv16, U, 16, S, w[:, 0:D], D, "F", mk_evac_qk(qT, qR, qnw_sb))
        proj(Uv16, U, 16, S, w[:, D:2 * D], D, "F", mk_evac_qk(kT, kR, knw_sb))

        def evac_v(f0, tb, ps, FB):
            buf = vst[(f0 // 256) % 2]
            P.cp("act", buf.ap()[:, tb, :], ps.ap()[:, :256], reads=[ps], acc=[buf])
            if tb == NB - 1:
                P.dma("sp", v_tok.rearrange("(tb p) f -> p tb f", p=128)[:, :, f0:f0 + 256], buf.ap(), reads=[buf], acc=[vR])

        proj(Uv16, U, 16, S, w[:, 2 * D:3 * D], D, "T", evac_v)

        def evac_g(f0, tt, ps, TW):
            qs = qst[(f0 // 128) % 2]
            P.act(qs.ap()[:, tt * TW:(tt + 1) * TW], ps.ap()[:, :TW], AF.Silu, reads=[ps], acc=[qs])
            if tt == S // TW - 1:
                P.dma("sp", sgT[f0:f0 + 128, :], qs.ap(), reads=[qs], acc=[sgR])

        proj(Uv16, U, 16, S, w[:, 3 * D:4 * D], D, "F", evac_g)

        P.stage_begin()
        NH = 4
        qh = [P.sb("qh", [128, S], BF16) for _ in range(NH)]
        kh = [P.sb("kh", [128, S], BF16) for _ in range(NH)]
        nqh = [P.sb("nqh", [128, S], BF16) for _ in range(NH)]
        vh = [P.sb("vh", [128, NB, 128], BF16) for _ in range(NH)]
        sgh = [P.sb("sgh", [128, S], BF16) for _ in range(NH)]
        NSL = 4

        class _View(Res):
            def ap(self):
                return self.h[:].bitcast(F32)

        pst_f = _View("pst_f", pst.h)
        R_f = [P.sb("R_f", [128, 512], F32) for _ in range(NSL)]
        R_b = [[P.sb("R_b", [128, 512], BF16) for _ in range(2)] for _ in range(NSL)]
        Eb = [P.sb("Eb", [128, 512], F32) for _ in range(NSL)]
        Lb = [P.sb("Lb", [128, 512], BF16) for _ in range(NSL)]
        attb = [P.sb("attb", [128, 512], BF16) for _ in range(NSL)]
        ogb = [P.sb("ogb", [128, 512], BF16) for _ in range(NSL)]
        Zp = [psb[0], psb[2], psb[4], psb[6]]
        Pp = Zp
        Op_ = [psb[1], psb[3], psb[5], pst_f]

        def load_head(h):
            b = h % NH
            r = slice(h * 128, (h + 1) * 128)
            P.dma("sp", qh[b].ap(), qT[r, :], reads=[qR], writes=[qh[b]])
            P.dma("sp", kh[b].ap(), kT[r, :], reads=[kR], writes=[kh[b]])
            P.dma("sp", sgh[b].ap(), sgT[r, :], reads=[sgR], writes=[sgh[b]])
            P.dma("sp", vh[b].ap(), v_tok.rearrange("(tb p) f -> p tb f", p=128)[:, :, r], reads=[vR], writes=[vh[b]])
            P.ts("pool", nqh[b].ap(), qh[b].ap(), -1.0, 0.0, ALU.mult, ALU.add, reads=[qh[b]], writes=[nqh[b]])

        def stream(sl, h, qc):
            b = h % NH
            qs_ = slice(qc * 512, (qc + 1) * 512)
            E, L, att, Z, Pq, O, Rf = Eb[sl], Lb[sl], attb[sl], Zp[sl], Pp[sl], Op_[sl], R_f[sl]
            first = True
            n = 0
            for sb_ in range(min(4 * qc + 3, NB - 1), -1, -1):
                r = sb_ - 4 * qc
                ks = slice(sb_ * 128, (sb_ + 1) * 128)
                P.mm(Z.ap(), kh[b].ap()[:, ks], qh[b].ap()[:, qs_], True, True, reads=[kh[b], qh[b]], writes=[Z])
                yield
                c0 = max(r, 0) * 128
                P.act(E.ap()[:, c0:], Z.ap()[:, c0:], AF.Exp, reads=[Z], writes=[E])
                P.act(L.ap()[:, c0:], E.ap()[:, c0:], AF.Ln, reads=[E], writes=[L], bias=1.0)
                if r >= 0:
                    if c0 > 0:
                        P.memset("pool", L.ap()[:, :c0], 0.0, acc=[L])
                    P.tt("pool", L.ap()[:, c0:c0 + 128], L.ap()[:, c0:c0 + 128], maskr[r].ap()[:, c0:c0 + 128], ALU.mult,
                         reads=[L, maskr[r]], writes=[L])
                yield
                P.mm(Pq.ap(), tincl_bf.ap(), L.ap(), True, False, reads=[tincl_bf, L], writes=[Pq])
                if not first:
                    rb_ = R_b[sl][n % 2]
                    P.mm(Pq.ap(), ones_bf.ap(), rb_.ap(), False, False, reads=[ones_bf, rb_], writes=[Pq])
                P.mm(Pq.ap(), kh[b].ap()[:, ks], nqh[b].ap()[:, qs_], False, True, reads=[kh[b], nqh[b]], writes=[Pq])
                yield
                P.act(att.ap()[:, c0:], Pq.ap()[:, c0:], AF.Exp, reads=[Pq], writes=[att], scale=-1.0)
                if r >= 0:
                    if c0 > 0:
                        P.memset("pool", att.ap()[:, :c0], 0.0, acc=[att])
                    P.tt("pool", att.ap()[:, c0:c0 + 128], att.ap()[:, c0:c0 + 128], maskr[r].ap()[:, c0:c0 + 128], ALU.mult,
                         reads=[att, maskr[r]], writes=[att])
                if sb_ > 0:
                    if first:
                        P.cp("dve", Rf.ap(), L.ap(), reads=[L], writes=[Rf])
                    else:
                        P.tt("dve", Rf.ap(), Rf.ap(), L.ap(), ALU.add, reads=[Rf, L], writes=[Rf])
                    n += 1
                    P.cp("dve", R_b[sl][n % 2].ap(), Rf.ap(), reads=[Rf], writes=[R_b[sl][n % 2]])
                yield
                P.mm(O.ap(), vh[b].ap()[:, sb_, :], att.ap(), first, sb_ == 0, reads=[vh[b], att], writes=[O])
                first = False
            og = ogb[sl]
            P.tt("dve", og.ap(), O.ap(), sgh[b].ap()[:, qs_], ALU.mult, reads=[O, sgh[b]], writes=[og])
            P.dma("sp", yT[h * 128:(h + 1) * 128, qs_], og.ap(), reads=[og], acc=[yTR])

        tasks = [(h, qc) for h in range(16) for qc in range(S // 512 - 1, -1, -1)]
        load_head(0)
        loaded = 0
        active = [None] * NSL
        ti = 0
        while True:
            for sl in range(NSL):
                if active[sl] is None and ti < len(tasks):
                    h, qc = tasks[ti]
                    ti += 1
                    if h + 1 < 16 and loaded < h + 1:
                        load_head(h + 1)
                        loaded = h + 1
                    active[sl] = stream(sl, h, qc)
            if all(a is None for a in active):
                break
            for sl in range(NSL):
                if active[sl] is not None:
                    try:
                        next(active[sl])
                    except StopIteration:
                        active[sl] = None

        P.stage_begin()
        for k4 in range(4):
            P.dma("sp", Uv16[:, k4 * 4:(k4 + 1) * 4, :],
                  yT.rearrange("(kc p) t -> p kc t", p=128)[:, k4 * 4:(k4 + 1) * 4, :], reads=[yTR], acc=[U])
        pre, evac = resid_hooks(h_in, hinR, h_out, houtR, 0)
        proj(Uv16, U, 16, S, sb_out_w[j], D, "F", evac, pre=pre)

    cur, curR = xT, win
    try:
        for n, i in enumerate(layers):
            last = n == len(layers) - 1
            if i % 2 == 0:
                ssd_layer(i // 2, cur, curR, hB, hBR, i * 16)
            else:
                sb_layer(i // 2, cur, curR, hB, hBR, i * 16)
            dst, dstR = (out, outR) if last else (hA, hAR)
            ple(i, hB, hBR, dst, dstR)
            cur, curR = hA, hAR
    except StopBuild:
        pass
    P.barrier()
    P.wait("sp", [outR])
    P.emit()
    return nc, P


def prep_shared(inp):
    f = lambda a: np.ascontiguousarray(np.asarray(a, dtype=np.float32))
    rep = lambda a: f(np.broadcast_to(np.asarray(a, dtype=np.float32).reshape(1, -1), (128, np.asarray(a).size)))
    return {
        "normw": f(np.asarray(inp["norm_w"]).reshape(4, 16, 128).transpose(2, 0, 1).reshape(128, 64)),
        "plenw": f(np.asarray(inp["ple_norm_w"]).reshape(4, 16, 128).transpose(2, 0, 1).reshape(128, 64)),
        "convw": f(np.asarray(inp["ssd_conv_w"]).reshape(2, 4, 48, 128).transpose(3, 0, 2, 1).reshape(128, 384)),
        "convb": f(np.asarray(inp["ssd_conv_b"]).reshape(2, 48, 128).transpose(2, 0, 1).reshape(128, 96)),
        "dtb": rep(inp["ssd_dt_bias"]),
        "alog": rep(inp["ssd_a_log"]),
        "dskip": rep(inp["ssd_d"]),
        "gnw": rep(inp["ssd_gnorm_w"]),
        "qnw": f(np.asarray(inp["sb_qn_w"]).T),
        "knw": f(np.asarray(inp["sb_kn_w"]).T),
        "ssd_in_w": f(inp["ssd_in_w"]),
        "ssd_out_w": f(inp["ssd_out_w"]),
        "sb_in_w": f(inp["sb_in_w"]),
        "sb_out_w": f(inp["sb_out_w"]),
        "gate_w": f(inp["ple_gate_w"]),
        "pproj_w": f(inp["ple_proj_w"]),
    }


def prep_core(inp, b, shared):
    m = dict(shared)
    m["xT"] = np.ascontiguousarray(np.asarray(inp["x"][b], dtype=np.float32).T)
    m["pT"] = np.ascontiguousarray(np.asarray(inp["p"], dtype=np.float32)[:, b].transpose(0, 2, 1))
    return m


_NC_CACHE = {}


def kernel(**inputs):
    x = np.asarray(inputs["x"])
    B, S, _ = x.shape
    key = (S,)
    if key not in _NC_CACHE:
        _NC_CACHE[key] = build(S=S)[0]
    nc = _NC_CACHE[key]
    shared = prep_shared(inputs)
    in_maps = [prep_core(inputs, b, shared) for b in range(B)]
    res = run_bass_kernel_spmd(nc, in_maps, core_ids=list(range(B)))
    outs = [np.asarray(r["out"]).T for r in res.results]
    return np.ascontiguousarray(np.stack(outs, axis=0).astype(np.float32))
```
